# Optimizing a Trainium2 kernel written in Bass

```python
import math
import jax, jax.numpy as jnp
from jax import lax
import numpy as np

D_MODEL = 2048
BATCH = 8
SEQ = 2048
DEPTH = 2

N_A_LAYERS = DEPTH // 2
N_B_LAYERS = DEPTH - N_A_LAYERS
GLA_HEADS = 4
GLA_DK = D_MODEL // 2
GLA_DV = D_MODEL
GLA_HK = GLA_DK // GLA_HEADS
GLA_HV = GLA_DV // GLA_HEADS
GATE_RANK = 16
GATE_TAU = 16.0
GLA_CHUNK = 64
GLA_IN = 2 * GLA_DK + 2 * GLA_DV + GATE_RANK
DIFF_HEADS = 8
DIFF_HD = D_MODEL // DIFF_HEADS // 2
Q_BLOCK = 128
LAMBDA_INIT_STD = 0.1
REL_BUCKETS = 32
REL_MAX_EXACT = REL_BUCKETS // 2
REL_MAX_DIST = 128
D_FF = ((8 * D_MODEL // 3 + 255) // 256) * 256
EPS = 1e-6

kernel_name = "yoco_gla_diffattn_hybrid"


def rmsnorm(x, g):
    xf = x.astype(jnp.float32)
    y = xf * lax.rsqrt(jnp.mean(xf * xf, axis=-1, keepdims=True) + EPS)
    return (y * g.astype(jnp.float32)).astype(x.dtype)


def swiglu(h, w_gate_up, w_down):
    gate, up = jnp.split(h @ w_gate_up, 2, axis=-1)
    return (jax.nn.silu(gate) * up) @ w_down


def _to_chunks(t, n_heads, head_dim):
    b, s, _ = t.shape
    return t.reshape(b, s // GLA_CHUNK, GLA_CHUNK, n_heads, head_dim).transpose(0, 3, 1, 2, 4)


def gla_mixer(h, w_in, w_fgate, b_fgate, g_norm, w_out):
    b, s, _ = h.shape
    f32 = jnp.float32
    proj = h @ w_in
    q, k, v, r, g_lr = jnp.split(
        proj, [GLA_DK, 2 * GLA_DK, 2 * GLA_DK + GLA_DV, 2 * GLA_DK + 2 * GLA_DV], axis=-1)
    log_a = jax.nn.log_sigmoid((g_lr @ w_fgate + b_fgate).astype(f32)) / GATE_TAU
    q = _to_chunks(q.astype(f32), GLA_HEADS, GLA_HK) * (GLA_HK ** -0.5)
    k = _to_chunks(k.astype(f32), GLA_HEADS, GLA_HK)
    v = _to_chunks(v.astype(f32), GLA_HEADS, GLA_HV)
    bcum = jnp.cumsum(_to_chunks(log_a, GLA_HEADS, GLA_HK), axis=-2)
    b_last = bcum[..., -1:, :]
    q_dec = q * jnp.exp(bcum)
    k_inv = k * jnp.exp(-bcum)
    k_end = k * jnp.exp(b_last - bcum)
    causal = jnp.tril(jnp.ones((GLA_CHUNK, GLA_CHUNK), dtype=bool))
    att = jnp.where(causal, jnp.einsum('bhncd,bhnjd->bhncj', q_dec, k_inv), 0.0)
    o_intra = jnp.einsum('bhncj,bhnje->bhnce', att, v)

    def step(state, xs):
        qd, ke, vc, dec = xs
        o = jnp.einsum('bhcd,bhde->bhce', qd, state)
        state = dec[..., None] * state + jnp.einsum('bhcd,bhce->bhde', ke, vc)
        return state, o

    xs = (jnp.moveaxis(q_dec, 2, 0), jnp.moveaxis(k_end, 2, 0), jnp.moveaxis(v, 2, 0),
          jnp.moveaxis(jnp.exp(b_last[..., 0, :]), 2, 0))
    state0 = jnp.zeros((b, GLA_HEADS, GLA_HK, GLA_HV), f32)
    _, o_inter = lax.scan(step, state0, xs)
    o = o_intra + jnp.moveaxis(o_inter, 0, 2)
    o = o.transpose(0, 2, 3, 1, 4).reshape(b, s, GLA_HEADS, GLA_HV)
    gate = jax.nn.silu(r.astype(f32)).reshape(b, s, GLA_HEADS, GLA_HV)
    o = rmsnorm(o, g_norm) * gate
    return o.reshape(b, s, GLA_DV).astype(h.dtype) @ w_out


def t5_bucket(dist):
    n = jnp.maximum(dist, 0)
    nf = jnp.maximum(n, 1).astype(jnp.float32)
    large = REL_MAX_EXACT + (jnp.log(nf / REL_MAX_EXACT) / math.log(REL_MAX_DIST / REL_MAX_EXACT)
                             * (REL_BUCKETS - REL_MAX_EXACT)).astype(jnp.int32)
    large = jnp.minimum(large, REL_BUCKETS - 1)
    return jnp.where(n < REL_MAX_EXACT, n, large)


def shared_kv(x, kv_norm_g, w_kv):
    b, s, _ = x.shape
    k, v = jnp.split(rmsnorm(x, kv_norm_g) @ w_kv, 2, axis=-1)
    k = k.astype(jnp.float32).reshape(b, s, DIFF_HEADS, 2, DIFF_HD).transpose(0, 2, 3, 1, 4)
    v = v.astype(jnp.float32).reshape(b, s, DIFF_HEADS, 2 * DIFF_HD).transpose(0, 2, 1, 3)
    return k, v


def diff_mixer(h, k_sh, v_sh, rel_bias_table, w_q, lam_q1, lam_k1, lam_q2, lam_k2,
               subln_g, w_out, lambda_init):
    b, s, _ = h.shape
    f32 = jnp.float32
    q = (h @ w_q).astype(f32).reshape(b, s, DIFF_HEADS, 2, DIFF_HD) * (DIFF_HD ** -0.5)
    n_blk = s // Q_BLOCK
    q_blocks = q.reshape(b, n_blk, Q_BLOCK, DIFF_HEADS, 2, DIFF_HD).transpose(1, 0, 3, 4, 2, 5)
    lam = (jnp.exp(jnp.sum(lam_q1.astype(f32) * lam_k1.astype(f32)))
           - jnp.exp(jnp.sum(lam_q2.astype(f32) * lam_k2.astype(f32))) + lambda_init)
    table = rel_bias_table.astype(f32)
    k_pos = jnp.arange(s)

    def attend(args):
        qb, start = args
        scores = jnp.einsum('bhmqd,bhmkd->bhmqk', qb, k_sh)
        dist = (start + jnp.arange(Q_BLOCK))[:, None] - k_pos[None, :]
        bias = jnp.take(table, t5_bucket(dist), axis=0)
        scores = scores + jnp.transpose(bias, (2, 0, 1))[None, :, None]
        scores = jnp.where(dist >= 0, scores, -jnp.inf)
        p = jax.nn.softmax(scores, axis=-1)
        a = p[:, :, 0] - lam * p[:, :, 1]
        return jnp.einsum('bhqk,bhke->bhqe', a, v_sh)

    o = lax.map(attend, (q_blocks, jnp.arange(n_blk) * Q_BLOCK))
    o = o.transpose(1, 0, 3, 2, 4).reshape(b, s, DIFF_HEADS, 2 * DIFF_HD)
    o = rmsnorm(o, subln_g) * (1.0 - lambda_init)
    return o.reshape(b, s, D_MODEL).astype(h.dtype) @ w_out


def setup_inputs(seed: int = 0) -> dict:
    key = jax.random.key(seed)
    ks = jax.random.split(key, 24)
    f32 = jnp.float32

    def nrm(k, shape, scale):
        return jax.random.normal(k, shape, f32) * scale

    def gain(k, shape):
        return 1.0 + 0.05 * jax.random.normal(k, shape, f32)

    return {
        "x": nrm(ks[0], (BATCH, SEQ, D_MODEL), 1.0),
        "rel_bias_table": nrm(ks[1], (REL_BUCKETS, DIFF_HEADS), 0.5),
        "kv_norm_g": gain(ks[2], (D_MODEL,)),
        "w_kv": nrm(ks[3], (D_MODEL, 2 * D_MODEL), D_MODEL ** -0.5),
        "gla_w_in": nrm(ks[4], (N_A_LAYERS, D_MODEL, GLA_IN), D_MODEL ** -0.5),
        "gla_w_fgate": nrm(ks[5], (N_A_LAYERS, GATE_RANK, GLA_DK), GATE_RANK ** -0.5),
        "gla_b_fgate": nrm(ks[6], (N_A_LAYERS, GLA_DK), 0.1),
        "gla_norm_g": gain(ks[7], (N_A_LAYERS, GLA_HV)),
        "gla_w_out": nrm(ks[8], (N_A_LAYERS, GLA_DV, D_MODEL), GLA_DV ** -0.5),
        "diff_w_q": nrm(ks[9], (N_B_LAYERS, D_MODEL, D_MODEL), D_MODEL ** -0.5),
        "diff_lam_q1": nrm(ks[10], (N_B_LAYERS, DIFF_HD), LAMBDA_INIT_STD),
        "diff_lam_k1": nrm(ks[11], (N_B_LAYERS, DIFF_HD), LAMBDA_INIT_STD),
        "diff_lam_q2": nrm(ks[12], (N_B_LAYERS, DIFF_HD), LAMBDA_INIT_STD),
        "diff_lam_k2": nrm(ks[13], (N_B_LAYERS, DIFF_HD), LAMBDA_INIT_STD),
        "diff_subln_g": gain(ks[14], (N_B_LAYERS, 2 * DIFF_HD)),
        "diff_w_out": nrm(ks[15], (N_B_LAYERS, D_MODEL, D_MODEL), D_MODEL ** -0.5),
        "pre_mix_g": gain(ks[16], (DEPTH, D_MODEL)),
        "post_mix_g": gain(ks[17], (DEPTH, D_MODEL)),
        "pre_ffn_g": gain(ks[18], (DEPTH, D_MODEL)),
        "post_ffn_g": gain(ks[19], (DEPTH, D_MODEL)),
        "ffn_w_gate_up": nrm(ks[20], (DEPTH, D_MODEL, 2 * D_FF), D_MODEL ** -0.5),
        "ffn_w_down": nrm(ks[21], (DEPTH, D_FF, D_MODEL), D_FF ** -0.5),
    }


def reference(x, rel_bias_table, kv_norm_g, w_kv, gla_w_in, gla_w_fgate, gla_b_fgate,
              gla_norm_g, gla_w_out, diff_w_q, diff_lam_q1, diff_lam_k1, diff_lam_q2,
              diff_lam_k2, diff_subln_g, diff_w_out, pre_mix_g, post_mix_g, pre_ffn_g,
              post_ffn_g, ffn_w_gate_up, ffn_w_down):
    k_sh = None
    v_sh = None
    for i in range(DEPTH):
        h = rmsnorm(x, pre_mix_g[i])
        if i < N_A_LAYERS:
            mix = gla_mixer(h, gla_w_in[i], gla_w_fgate[i], gla_b_fgate[i],
                            gla_norm_g[i], gla_w_out[i])
        else:
            if i == N_A_LAYERS:
                k_sh, v_sh = shared_kv(x, kv_norm_g, w_kv)
            j = i - N_A_LAYERS
            lambda_init = 0.8 - 0.6 * math.exp(-0.3 * i)
            mix = diff_mixer(h, k_sh, v_sh, rel_bias_table, diff_w_q[j], diff_lam_q1[j],
                             diff_lam_k1[j], diff_lam_q2[j], diff_lam_k2[j], diff_subln_g[j],
                             diff_w_out[j], lambda_init)
        x = x + rmsnorm(mix, post_mix_g[i])
        f = swiglu(rmsnorm(x, pre_ffn_g[i]), ffn_w_gate_up[i], ffn_w_down[i])
        x = x + rmsnorm(f, post_ffn_g[i])
    return x
```

```python
import contextlib
import math
import numpy as np
import concourse.bass as bass
import concourse.mybir as mybir
from concourse.bass_utils import run_bass_kernel_spmd

F32 = mybir.dt.float32
BF16 = mybir.dt.bfloat16
AF = mybir.ActivationFunctionType
ALU = mybir.AluOpType

S = 2048
D = 2048
NT = S // 128
KT = D // 128
DFF = 5632
FT = DFF // 128
GLA_IN = 6160
EPS = 1e-6
LAMBDA_INIT = 0.8 - 0.6 * math.exp(-0.3 * 1)
MASKV = -30000.0


class Tl:
    __slots__ = ("ap", "w", "r", "dsem", "psum")

    def __init__(self, ap):
        self.ap = ap
        self.w = None
        self.r = []
        self.dsem = None
        self.psum = "PSum" in type(ap.tensor).__name__

    def __getitem__(self, k):
        return self.ap[k]


class Op:
    __slots__ = ("q", "fn", "deps", "seq", "need", "dma", "dsem", "dval")

    def __init__(self, q, fn, dma):
        self.q = q
        self.fn = fn
        self.deps = {}
        self.seq = None
        self.need = False
        self.dma = dma
        self.dsem = None
        self.dval = None


QUEUES = ("pe", "act", "dve", "pool", "sp")
import os as _os
STRICT = _os.environ.get("KSTRICT", "0") == "1"


class Prog:
    def __init__(self, nc):
        self.nc = nc
        self.ops = {q: [] for q in QUEUES}
        self.qsem = {q: nc.alloc_semaphore("q_" + q) for q in QUEUES}
        self.qcount = {q: 0 for q in QUEUES}
        self.seen = {q: {} for q in QUEUES}
        self.free_dsems = {"hw": [nc.alloc_semaphore("dh%d" % i) for i in range(45)],
                           "sw": [nc.alloc_semaphore("ds%d" % i) for i in range(45)]}
        self.semval = {}
        self.tiles = []
        self.phase_dsems = []
        self.stacks = []
        self.n_inst = 0
        self.uid = 0

    def scope(self):
        st = contextlib.ExitStack()
        self.stacks.append(st)
        return st

    def end_scope(self):
        self.stacks.pop().close()

    def sb(self, shape, dtype, name=None):
        self.uid += 1
        h = self.stacks[-1].enter_context(self.nc.sbuf_tensor("%s_%d" % (name or "t", self.uid), list(shape), dtype))
        return h

    def T(self, ap):
        t = Tl(ap)
        self.tiles.append(t)
        return t

    def sbT(self, shape, dtype, name=None):
        h = self.sb(shape, dtype, name)
        return self.T(h[tuple(slice(None) for _ in shape)])

    def add(self, q, fn, reads=(), writes=(), dma=False):
        op = Op(q, fn, dma)
        for t in reads:
            if t.w is not None:
                op.deps[t.w] = True
            if t.psum:
                for r in t.r:
                    if r.q != q:
                        op.deps.setdefault(r, False)
        for t in writes:
            if t.w is not None:
                op.deps.setdefault(t.w, False)
            for r in t.r:
                op.deps.setdefault(r, False)
        if dma:
            st = None
            for t in list(writes) + list(reads):
                if t.dsem is not None or _is_sbuf(t):
                    st = t
                    break
            assert st is not None
            kind = "sw" if q == "pool" else "hw"
            if st.dsem is None:
                st.dsem = {}
            if kind not in st.dsem:
                sem_ = self.free_dsems[kind].pop()
                st.dsem[kind] = sem_
                self.phase_dsems.append((st, kind, sem_))
                self.semval.setdefault(sem_, 0)
            sem_ = st.dsem[kind]
            self.semval[sem_] += 16
            op.dsem = sem_
            op.dval = self.semval[sem_]
        for t in reads:
            t.r.append(op)
        for t in writes:
            t.w = op
            t.r = []
        self.ops[q].append(op)
        return op

    def flush(self):
        nc = self.nc
        drain = [(sem, self.semval[sem]) for (_, _, sem) in self.phase_dsems]
        for q in QUEUES:
            for op in self.ops[q]:
                for d, raw in op.deps.items():
                    if d.dma:
                        continue
                    if d.q == op.q and not op.dma and not raw and not (STRICT and op.q != "pe"):
                        continue
                    d.need = True
        for q in QUEUES:
            for op in self.ops[q]:
                if op.need and not op.dma:
                    self.qcount[q] += 1
                    op.seq = self.qcount[q]
        engs = {"pe": "tensor", "act": "scalar", "dve": "vector", "pool": "gpsimd", "sp": "sync"}

        def emit(q, eng):
            seen = self.seen[q]
            for op in self.ops[q]:
                w = {}
                for d, raw in op.deps.items():
                    if d.dma:
                        sem, val = d.dsem, d.dval
                    else:
                        if d.q == op.q and not op.dma and not raw and not (STRICT and op.q != "pe"):
                            continue
                        sem, val = self.qsem[d.q], d.seq
                    if w.get(sem, 0) < val:
                        w[sem] = val
                for sem, val in w.items():
                    if seen.get(sem, 0) < val:
                        eng.wait_ge(sem, val)
                        seen[sem] = val
                ins = op.fn(eng)
                self.n_inst += 1
                if op.dma:
                    ins.then_inc(op.dsem, 16)
                elif op.need:
                    ins.then_inc(self.qsem[q], 1)
            if q == "sp":
                for sem, val in drain:
                    if seen.get(sem, 0) < val:
                        eng.wait_ge(sem, val)
                        seen[sem] = val

        with nc.Block() as blk:
            for q in QUEUES:
                if self.ops[q] or q == "sp":
                    getattr(blk, engs[q])(lambda eng, q=q: emit(q, eng))
        for q in QUEUES:
            self.ops[q] = []
        for t, kind, sem in self.phase_dsems:
            t.dsem = None
            self.free_dsems[kind].append(sem)
        self.phase_dsems = []
        for t in self.tiles:
            t.w = None
            t.r = []
        self.tiles = [t for t in self.tiles if not _is_sbuf(t) or True]


def _is_sbuf(t):
    return "SBTensor" in type(t.ap.tensor).__name__


def dma(P, q, out_t, out_ap, in_t, in_ap):
    return P.add(q, lambda e: e.dma_start(out=out_ap, in_=in_ap), reads=[in_t], writes=[out_t], dma=True)


def mm(P, ps_t, out_ap, lhsT_t, lhsT_ap, rhs_t, rhs_ap, start, stop):
    return P.add("pe", lambda e: e.matmul(out_ap, lhsT=lhsT_ap, rhs=rhs_ap, start=start, stop=stop),
                 reads=[lhsT_t, rhs_t], writes=[ps_t])


def tr(P, ps_t, out_ap, in_t, in_ap, id_t, id_ap):
    return P.add("pe", lambda e: e.transpose(out=out_ap, in_=in_ap, identity=id_ap), reads=[in_t, id_t], writes=[ps_t])


def act(P, out_t, out_ap, in_t, in_ap, func, bias=None, scale=None, accum=None, extra_reads=(), q="act"):
    kw = {}
    if bias is not None:
        kw["bias"] = bias
    if scale is not None:
        kw["scale"] = scale
    wr = [out_t]
    if accum is not None:
        kw["accum_out"] = accum[1]
        wr.append(accum[0])
    return P.add("act", lambda e: e.activation(out=out_ap, in_=in_ap, func=func, **kw),
                 reads=[in_t] + list(extra_reads), writes=wr)


def tt(P, q, out_t, out_ap, a_t, a_ap, b_t, b_ap, op):
    return P.add(q, lambda e: e.tensor_tensor(out=out_ap, in0=a_ap, in1=b_ap, op=op), reads=[a_t, b_t], writes=[out_t])


def ts(P, q, out_t, out_ap, a_t, a_ap, s1, s2, op0, op1=None, extra_reads=()):
    if op1 is None:
        return P.add(q, lambda e: e.tensor_scalar(out=out_ap, in0=a_ap, scalar1=s1, scalar2=None, op0=op0),
                     reads=[a_t] + list(extra_reads), writes=[out_t])
    return P.add(q, lambda e: e.tensor_scalar(out=out_ap, in0=a_ap, scalar1=s1, scalar2=s2, op0=op0, op1=op1),
                 reads=[a_t] + list(extra_reads), writes=[out_t])


def stt(P, out_t, out_ap, a_t, a_ap, scalar, b_t, b_ap, op0, op1, extra_reads=()):
    return P.add("dve", lambda e: e.scalar_tensor_tensor(out=out_ap, in0=a_ap, scalar=scalar, in1=b_ap, op0=op0, op1=op1),
                 reads=[a_t, b_t] + list(extra_reads), writes=[out_t])


def cp(P, q, out_t, out_ap, in_t, in_ap):
    if q == "act":
        return P.add("act", lambda e: e.activation(out=out_ap, in_=in_ap, func=AF.Copy), reads=[in_t], writes=[out_t])
    return P.add(q, lambda e: e.tensor_copy(out=out_ap, in_=in_ap), reads=[in_t], writes=[out_t])


def memset(P, q, t, ap, val):
    return P.add(q, lambda e: e.memset(ap, val), writes=[t])


def rstd_ops(P, st_t, ss_ap, tmp_ap, out_ap, n, nh_t, nh_ap):
    ts(P, "pool", st_t, tmp_ap, st_t, ss_ap, 1.0 / n, EPS, ALU.mult, ALU.add)
    P.add("pool", lambda e: e.tensor_tensor(out=out_ap, in0=tmp_ap, in1=nh_ap, op=ALU.pow), reads=[st_t, nh_t], writes=[st_t])


class Rot:
    def __init__(self, items):
        self.items = items
        self.i = 0

    def next(self):
        t = self.items[self.i % len(self.items)]
        self.i += 1
        return t


def build_nc(debug=False):
    import os
    SKIP = set(os.environ.get("KSKIP", "").split(",")) if debug else set()
    nc = bass.Bass("TRN2", target_bir_lowering=False)
    P = Prog(nc)

    def din(name, shape, dt=F32):
        return nc.dram_tensor(name, list(shape), dt, kind="ExternalInput")

    def dscr(name, shape, dt=F32, out=False):
        return nc.dram_tensor(name, list(shape), dt, kind=("ExternalOutput" if out else "Internal"))

    x_in = din("x", [S, D])
    bias_t = din("bias_t", [128, 8, 256])
    ctab = din("ctab", [8])
    kv_norm_g = din("kv_norm_g", [D])
    w_kv = din("w_kv", [D, 2 * D])
    gla_w_in = din("gla_w_in", [D, GLA_IN])
    gla_w_fgate = din("gla_w_fgate", [16, 1024])
    gla_b_fgate = din("gla_b_fgate", [1024])
    gla_norm_g = din("gla_norm_g", [512])
    gla_w_out = din("gla_w_out", [D, D])
    diff_w_q = din("diff_w_q", [D, D])
    lam4 = din("lam4", [4, 128])
    diff_subln_g = din("diff_subln_g", [256])
    diff_w_out = din("diff_w_out", [D, D])
    pre_mix_g = din("pre_mix_g", [2, D])
    post_mix_g = din("post_mix_g", [2, D])
    pre_ffn_g = din("pre_ffn_g", [2, D])
    post_ffn_g = din("post_ffn_g", [2, D])
    ffn_w_gate_up = din("ffn_w_gate_up", [2, D, 2 * DFF])
    ffn_w_down = din("ffn_w_down", [2, DFF, D])

    x1 = dscr("x1", [S, D], out=debug)
    x2 = dscr("x2", [S, D], out=debug)
    x3 = dscr("x3", [S, D], out=debug)
    y = dscr("y", [S, D], out=True)
    fscr = dscr("fscr", [S, D])
    qT_s = fscr[0:1024, :]
    kT_s = fscr[1024:2048, :]
    v_s = dscr("v_s", [S, D], BF16)
    sr_s = dscr("sr_s", [S, D], BF16)
    glr_s = dscr("glr_s", [16, S])
    ogT_s = dscr("ogT_s", [D, S], BF16)
    kT2 = dscr("kT2", [D, S], BF16)
    qT2 = dscr("qT2", [D, S], BF16)
    v2 = dscr("v2", [S, D], BF16)
    oT_s = dscr("oT_s", [D, S], BF16)

    def rowtiles(h):
        return [P.T(h[i * 128:(i + 1) * 128, :]) for i in range(h.shape[0] // 128)]

    xin_t = rowtiles(x_in)
    x1_t, x2_t, x3_t, y_t = rowtiles(x1), rowtiles(x2), rowtiles(x3), rowtiles(y)
    WT = P.T(gla_w_in[:, :])
    qTs_t, kTs_t = rowtiles(qT_s), rowtiles(kT_s)
    vs_t, srs_t = rowtiles(v_s), rowtiles(sr_s)
    glrs_t = P.T(glr_s[:, :])
    ogTs_t = P.T(ogT_s[:, :])
    kT2_t, qT2_t = rowtiles(kT2), rowtiles(qT2)
    v2_t = rowtiles(v2)
    oTs_t = P.T(oT_s[:, :])
    fscr_t = rowtiles(fscr)

    g0 = P.scope()
    ps = [P.T(nc.alloc_psum_tensor("ps%d" % i, [128, 512], F32)[:, :]) for i in range(8)]
    psb = [t.ap.tensor.bitcast(BF16) for t in ps]
    ident = P.sbT([128, 128], BF16, "ident")
    identf = P.sbT([128, 128], F32, "identf")
    ones_b = P.sbT([128, 128], BF16, "ones_b")
    ones_f = P.sbT([128, 128], F32, "ones_f")
    maskT = P.sbT([128, 128], F32, "maskT")
    nh = P.sbT([128, 512], F32, "nh")
    neglam = P.sbT([128, 4], F32, "neglam")
    gsub = P.sbT([128, 2], F32, "gsub")
    lamt = P.sbT([128, 4], F32, "lamt")
    ctab_b = P.sbT([128, 8], F32, "ctab_b")
    _fs_h = P.sb([128, NT * 8], F32, "fstats")
    fstats = [P.T(_fs_h[:, i * 8:(i + 1) * 8]) for i in range(NT)]

    memset(P, "pool", identf, identf[:, :], 1.0)
    P.add("pool", lambda e: e.affine_select(out=identf[:, :], in_=identf[:, :], pattern=[[1, 128]], compare_op=ALU.is_equal,
                                             fill=0.0, base=0, channel_multiplier=-1), reads=[identf], writes=[identf])
    cp(P, "pool", ident, ident[:, :], identf, identf[:, :])
    memset(P, "pool", ones_b, ones_b[:, :], 1.0)
    memset(P, "pool", ones_f, ones_f[:, :], 1.0)
    memset(P, "pool", nh, nh[:, :], -0.5)
    memset(P, "pool", maskT, maskT[:, :], 1.0)
    P.add("pool", lambda e: e.affine_select(out=maskT[:, :], in_=maskT[:, :], pattern=[[1, 128]], compare_op=ALU.is_ge,
                                             fill=0.0, base=0, channel_multiplier=-1), reads=[maskT], writes=[maskT])
    for i in range(4):
        dma(P, "sp", lamt, lamt[:, i:i + 1], WT, lam4[i, :].rearrange("(p o) -> p o", o=1))
    for i in range(2):
        dma(P, "sp", gsub, gsub[:, i:i + 1], WT, diff_subln_g[i * 128:(i + 1) * 128].rearrange("(p o) -> p o", o=1))
    dma(P, "sp", ctab_b, ctab_b[:, :], WT, bass.AP(ctab, 0, [[0, 128], [1, 8]]))
    tt(P, "dve", neglam, neglam[:, 0:1], lamt, lamt[:, 0:1], lamt, lamt[:, 1:2], ALU.mult)
    tt(P, "dve", neglam, neglam[:, 1:2], lamt, lamt[:, 2:3], lamt, lamt[:, 3:4], ALU.mult)
    mm(P, ps[0], ps[0][:, 0:2], ones_f, ones_f[:, :], neglam, neglam[:, 0:2], True, True)
    act(P, neglam, neglam[:, 2:4], ps[0], ps[0][:, 0:2], AF.Exp)
    tt(P, "dve", neglam, neglam[:, 0:1], neglam, neglam[:, 3:4], neglam, neglam[:, 2:3], ALU.subtract)
    ts(P, "dve", neglam, neglam[:, 0:1], neglam, neglam[:, 0:1], -LAMBDA_INIT, None, ALU.add)
    ts(P, "dve", gsub, gsub[:, 0:2], gsub, gsub[:, 0:2], 1.0 - LAMBDA_INIT, None, ALU.mult)
    P.flush()

    def load_gb(g_ap_1d):
        gb = P.sbT([128, D], F32, "gb")
        n = g_ap_1d.shape[0]
        dma(P, "sp", gb, gb[:, 0:n], WT, bass.AP(g_ap_1d.tensor, g_ap_1d.offset, [[0, 128], [1, n]]))
        return gb

    def stat_tiles(n, w=4):
        h = P.sb([128, n * w], F32, "stat")
        return [P.T(h[:, i * w:(i + 1) * w]) for i in range(n)]

    def chunk_tiles(h, ncols):
        return [P.T(h[:, :, c * 512:(c + 1) * 512]) for c in range(ncols // 512)]

    def prenorm_T(xtiles, gb, hT, hTc, nt, xpool, xspool, junk, stats, psr, fuse=None):
        xs_of = {}

        def stage_a(li):
            xt = xpool.next()
            st = stats[li]
            dma(P, "sp", xt, xt[:, :], xtiles[li], xtiles[li][:, :])
            if fuse is not None:
                ft = fuse["fpool"].next()
                dma(P, "sp", ft, ft[:, :], fuse["f"][li], fuse["f"][li][:, :])
                fs = fuse["stats"][li]
                gp = fuse["gpost"]
                tt(P, "pool", fs, fs[:, 4:5], fs, fs[:, 0:1], fs, fs[:, 1:2], ALU.add)
                tt(P, "pool", fs, fs[:, 5:6], fs, fs[:, 2:3], fs, fs[:, 3:4], ALU.add)
                tt(P, "pool", fs, fs[:, 4:5], fs, fs[:, 4:5], fs, fs[:, 5:6], ALU.add)
                rstd_ops(P, fs, fs[:, 4:5], fs[:, 6:7], fs[:, 7:8], D, nh, nh[:, 0:1])
                stt(P, ft, ft[:, :], ft, ft[:, :], fs[:, 7:8], gp, gp[:, :], ALU.mult, ALU.mult, extra_reads=[fs])
                tt(P, "dve", xt, xt[:, :], xt, xt[:, :], ft, ft[:, :], ALU.add)
                dma(P, "pool", fuse["xnew"][li], fuse["xnew"][li][:, :], xt, xt[:, :])
            act(P, junk, junk[:, :], xt, xt[:, :], AF.Square, accum=(st, st[:, 0:1]))
            rstd_ops(P, st, st[:, 0:1], st[:, 1:2], st[:, 2:3], D, nh, nh[:, 0:1])
            xs = xspool.next()
            stt(P, xs, xs[:, :], xt, xt[:, :], st[:, 2:3], gb, gb[:, :], ALU.mult, ALU.mult, extra_reads=[st])
            xs_of[li] = xs

        def stage_b(li):
            xs = xs_of.pop(li)
            for half in range(2):
                pi = psr.next()
                pt, pb = ps[pi], psb[pi]
                for k in range(8):
                    kt = half * 8 + k
                    tr(P, pt, pb[:, k * 128:(k + 1) * 128], xs, xs[:, kt * 128:(kt + 1) * 128], ident, ident[:, :])
                q = "act" if half == 0 else "dve"
                cp(P, q, hTc[li // 4], hT[:, half * 8:half * 8 + 8, li * 128:(li + 1) * 128], pt,
                   pb[:, 0:1024].rearrange("p (a b) -> p a b", b=128))

        stage_a(0)
        for li in range(nt):
            if li + 1 < nt:
                stage_a(li + 1)
            stage_b(li)

    def wslab(W2d, c0, ncols, slab, kts):
        src = W2d.rearrange("(kt p) n -> p kt n", p=128)[:, :, c0:c0 + ncols]
        dma(P, "pool", slab, slab[:, 0:kts, 0:ncols], WT, src)

    def phase_A():
        P.scope()
        w_in = gla_w_in[:, :]
        gb = load_gb(pre_mix_g[0, :])
        hT = P.sb([128, KT, S], BF16, "hT")
        hTc = chunk_tiles(hT, S)
        xpool = Rot([P.sbT([128, D], F32, "xt") for _ in range(2)])
        xspool = Rot([P.sbT([128, D], BF16, "xs") for _ in range(3)])
        junk = P.sbT([128, D], BF16, "junk")
        prenorm_T(xin_t, gb, hT, hTc, NT, xpool, xspool, junk, stat_tiles(NT), Rot([0, 1]))
        slabsB = Rot([P.sbT([128, KT, 256], BF16, "slabB") for _ in range(2)])
        evB = Rot([P.sbT([128, S], F32, "evB") for _ in range(2)])
        psr = Rot([2, 3, 4, 5, 6, 7])
        for sidx in range(8):
            slab = slabsB.next()
            wslab(w_in, sidx * 256, 256, slab, KT)
            for n in range(2):
                nt_idx = sidx * 2 + n
                ev = evB.next()
                for c in range(4):
                    pt = ps[psr.next()]
                    for kt in range(KT):
                        mm(P, pt, pt[:, :], slab, slab[:, kt, n * 128:(n + 1) * 128], hTc[c], hT[:, kt, c * 512:(c + 1) * 512],
                           kt == 0, kt == KT - 1)
                    cp(P, "act" if c % 2 == 0 else "dve", ev, ev[:, c * 512:(c + 1) * 512], pt, pt[:, :])
                dst = qTs_t[nt_idx] if nt_idx < 8 else kTs_t[nt_idx - 8]
                dma(P, "sp", dst, dst[:, :], ev, ev[:, :])
        slab = slabsB.next()
        wslab(w_in, 6144, 16, slab, KT)
        ev = evB.next()
        for c in range(4):
            pt = ps[psr.next()]
            for kt in range(KT):
                mm(P, pt, pt[0:16, :], slab, slab[:, kt, 0:16], hTc[c], hT[:, kt, c * 512:(c + 1) * 512], kt == 0, kt == KT - 1)
            cp(P, "act", ev, ev[0:16, c * 512:(c + 1) * 512], pt, pt[0:16, :])
        dma(P, "sp", glrs_t, glrs_t[:, :], ev, ev[0:16, :])
        slabsA = Rot([P.sbT([128, KT, 512], BF16, "slabA") for _ in range(2)])
        evA = Rot([P.sbT([128, 512], BF16, "evA") for _ in range(4)])
        for ch in range(8):
            slab = slabsA.next()
            wslab(w_in, 2048 + ch * 512, 512, slab, KT)
            is_v = ch < 4
            for t in range(NT):
                pt = ps[psr.next()]
                for kt in range(KT):
                    mm(P, pt, pt[:, :], hTc[t // 4], hT[:, kt, t * 128:(t + 1) * 128], slab, slab[:, kt, :], kt == 0, kt == KT - 1)
                ev = evA.next()
                if is_v:
                    cp(P, "dve", ev, ev[:, :], pt, pt[:, :])
                    dst = vs_t[t]
                    dma(P, "sp", dst, dst[:, ch * 512:(ch + 1) * 512], ev, ev[:, :])
                else:
                    act(P, ev, ev[:, :], pt, pt[:, :], AF.Silu)
                    dst = srs_t[t]
                    dma(P, "sp", dst, dst[:, (ch - 4) * 512:(ch - 3) * 512], ev, ev[:, :])
        P.flush()
        P.end_scope()

    def phase_B():
        P.scope()
        qd = [P.sbT([128, S], BF16, "qd") for _ in range(8)]
        ki = [P.sbT([128, S], BF16, "ki") for _ in range(8)]
        ke_h = P.sb([128, NT, 1024], BF16, "ke")
        ke = [P.T(ke_h[:, t, :]) for t in range(NT)]
        dec = [P.sbT([128, 16], F32, "dec") for _ in range(8)]
        P.scope()
        glr = P.sbT([32, S], F32, "glr")
        wfb = P.sbT([32, 1024], F32, "wfb")
        rmask = P.sbT([128, S], BF16, "rmask")
        memset(P, "pool", glr, glr[:, :], 1.0)
        dma(P, "sp", glr, glr[0:16, :], glrs_t, glrs_t[:, :])
        dma(P, "sp", wfb, wfb[0:16, :], WT, gla_w_fgate[:, :])
        dma(P, "sp", wfb, wfb[16:17, :], WT, gla_b_fgate.ap().rearrange("(o n) -> o n", o=1))
        memset(P, "pool", rmask, rmask[:, :], 1.0)
        memset(P, "pool", rmask, rmask[:, 0:S:128], 0.0)
        lt = Rot([P.sbT([128, S], F32, "lt") for _ in range(1)])
        cumr = Rot([P.sbT([128, S], F32, "cum") for _ in range(1)])
        e1r = Rot([P.sbT([128, S], F32, "e1") for _ in range(2)])
        e2r = Rot([P.sbT([128, S], F32, "e2") for _ in range(2)])
        qtr = Rot([P.sbT([128, S], F32, "qt") for _ in range(2)])
        ktr = Rot([P.sbT([128, S], F32, "kt") for _ in range(2)])
        keTr = Rot([P.sbT([128, S], BF16, "keT") for _ in range(2)])
        psr = Rot([0, 1, 2, 3])
        psr2 = Rot([4, 5, 6, 7])
        ee = {}

        def b1_gate(dt):
            l = lt.next()
            for c in range(4):
                pt = ps[psr.next()]
                mm(P, pt, pt[:, :], wfb, wfb[0:17, dt * 128:(dt + 1) * 128], glr, glr[0:17, c * 512:(c + 1) * 512], True, True)
                act(P, l, l[:, c * 512:(c + 1) * 512], pt, pt[:, :], AF.Exp, scale=-1.0)
            act(P, l, l[:, :], l, l[:, :], AF.Ln, bias=1.0)
            cum = cumr.next()
            P.add("dve", lambda e, cum=cum, l=l: e.tensor_tensor_scan(out=cum[:, :], data0=rmask[:, :], data1=l[:, :], initial=0.0,
                                                                      op0=ALU.mult, op1=ALU.add), reads=[rmask, l], writes=[cum])
            e1, e2 = e1r.next(), e2r.next()
            act(P, e1, e1[:, :], cum, cum[:, :], AF.Exp, scale=-1.0 / 16)
            act(P, e2, e2[:, :], cum, cum[:, :], AF.Exp, scale=1.0 / 16)
            cp(P, "pool", dec[dt], dec[dt][:, :], e1, e1[:, 127:S:128])
            qt, kt_ = qtr.next(), ktr.next()
            dma(P, "sp", qt, qt[:, :], qTs_t[dt], qTs_t[dt][:, :])
            dma(P, "sp", kt_, kt_[:, :], kTs_t[dt], kTs_t[dt][:, :])
            ee[dt] = (e1, e2, qt, kt_)

        def b1_apply(dt):
            e1, e2, qt, kt_ = ee.pop(dt)
            stt(P, qd[dt], qd[dt][:, :], qt, qt[:, :], 1.0 / 16, e1, e1[:, :], ALU.mult, ALU.mult)
            tt(P, "dve", kt_, kt_[:, :], kt_, kt_[:, :], e2, e2[:, :], ALU.mult)
            cp(P, "pool", ki[dt], ki[dt][:, :], kt_, kt_[:, :])
            keT = keTr.next()
            for t in range(NT):
                ts(P, "dve", keT, keT[:, t * 128:(t + 1) * 128], kt_, kt_[:, t * 128:(t + 1) * 128],
                   dec[dt][:, t:t + 1], None, ALU.mult, extra_reads=[dec[dt]])
            for half in range(2):
                pi = psr2.next()
                for k8 in range(8):
                    t = half * 8 + k8
                    tr(P, ps[pi], psb[pi][:, k8 * 128:(k8 + 1) * 128], keT, keT[:, t * 128:(t + 1) * 128], ident, ident[:, :])
                P.add("act", lambda e, pi=pi, half=half, dt=dt: e.activation(
                    out=ke_h[:, half * 8:half * 8 + 8, dt * 128:(dt + 1) * 128],
                    in_=psb[pi][:, 0:1024].rearrange("p (a b) -> p a b", b=128), func=AF.Copy),
                    reads=[ps[pi]], writes=ke[half * 8:half * 8 + 8])

        b1_gate(0)
        for dt in range(8):
            if dt + 1 < 8:
                b1_gate(dt + 1)
            b1_apply(dt)
        P.flush()
        P.end_scope()
        P.scope()
        gnb = P.sbT([128, 512], F32, "gnb")
        dma(P, "sp", gnb, gnb[:, :], WT, bass.AP(gla_norm_g, 0, [[0, 128], [1, 512]]))
        vr = Rot([P.sbT([128, D], BF16, "v") for _ in range(3)])
        srr = Rot([P.sbT([128, D], BF16, "sr") for _ in range(3)])
        state = [[P.sbT([128, 512], F32, "st") for _ in range(2)] for _ in range(4)]
        stb = [[Rot([P.sbT([128, 512], BF16, "stb") for _ in range(2)]) for _ in range(2)] for _ in range(4)]
        stb_cur = [[None, None] for _ in range(4)]
        attm_r = Rot([P.sbT([128, 128], BF16, "attm") for _ in range(4)])
        tmp_r = Rot([P.sbT([128, 512], F32, "tmp") for _ in range(3)])
        og_r = Rot([P.sbT([128, D], BF16, "og") for _ in range(2)])
        ogT_r = Rot([P.sbT([128, KT, 128], BF16, "ogT") for _ in range(2)])
        junk = P.sbT([128, 512], BF16, "junk")
        stats = stat_tiles(64)
        ps_att = Rot([0])
        ps_o = Rot([1, 2, 7])
        ps_su = Rot([3, 4])
        ps_tr = Rot([5, 6])
        ogT_view = ogT_s.ap().rearrange("(et p) s -> p et s", p=128)
        items = [(t, h) for t in range(NT) for h in range(4)]
        vt, srt, ogt, att_of = {}, {}, {}, {}

        def load(t):
            v, sr = vr.next(), srr.next()
            dma(P, "sp", v, v[:, :], vs_t[t], vs_t[t][:, :])
            dma(P, "sp", sr, sr[:, :], srs_t[t], srs_t[t][:, :])
            vt[t], srt[t], ogt[t] = v, sr, og_r.next()

        def att_stage(i):
            t, h = items[i]
            tc = slice(t * 128, (t + 1) * 128)
            pa = ps[ps_att.next()]
            for dl in range(2):
                dt = 2 * h + dl
                mm(P, pa, pa[:, 0:128], ki[dt], ki[dt][:, tc], qd[dt], qd[dt][:, tc], dl == 0, dl == 1)
            attm = attm_r.next()
            tt(P, "dve", attm, attm[:, :], pa, pa[:, 0:128], maskT, maskT[:, :], ALU.mult)
            att_of[i] = attm

        def main_stage(i):
            t, h = items[i]
            tc = slice(t * 128, (t + 1) * 128)
            hc = slice(h * 512, (h + 1) * 512)
            v = vt[t]
            attm = att_of.pop(i)
            po = ps[ps_o.next()]
            mm(P, po, po[:, :], attm, attm[:, :], v, v[:, hc], True, t == 0)
            if t > 0:
                for dl in range(2):
                    dt = 2 * h + dl
                    sb_ = stb_cur[h][dl]
                    mm(P, po, po[:, :], qd[dt], qd[dt][:, tc], sb_, sb_[:, :], False, dl == 1)
            if t < NT - 1:
                for dl in range(2):
                    dt = 2 * h + dl
                    pu = ps[ps_su.next()]
                    mm(P, pu, pu[:, :], ke[t], ke_h[:, t, dt * 128:(dt + 1) * 128], v, v[:, hc], True, True)
                    st_ = state[h][dl]
                    if t == 0:
                        cp(P, "dve", st_, st_[:, :], pu, pu[:, :])
                    else:
                        stt(P, st_, st_[:, :], st_, st_[:, :], dec[dt][:, t:t + 1], pu, pu[:, :], ALU.mult, ALU.add,
                            extra_reads=[dec[dt]])
                    nb = stb[h][dl].next()
                    cp(P, "act", nb, nb[:, :], st_, st_[:, :])
                    stb_cur[h][dl] = nb
            return po

        def epi_a(i, po):
            st = stats[i]
            act(P, junk, junk[:, :], po, po[:, :], AF.Square, accum=(st, st[:, 0:1]))
            rstd_ops(P, st, st[:, 0:1], st[:, 1:2], st[:, 2:3], 512, nh, nh[:, 0:1])

        def epi_stage(i, po):
            t, h = items[i]
            hc = slice(h * 512, (h + 1) * 512)
            st = stats[i]
            og, sr = ogt[t], srt[t]
            tmp = tmp_r.next()
            stt(P, tmp, tmp[:, :], po, po[:, :], st[:, 2:3], gnb, gnb[:, :], ALU.mult, ALU.mult, extra_reads=[st])
            tt(P, "pool", og, og[:, hc], tmp, tmp[:, :], sr, sr[:, hc], ALU.mult)
            if h == 3:
                tc = slice(t * 128, (t + 1) * 128)
                ogT = ogT_r.next()
                for half in range(2):
                    pi = ps_tr.next()
                    for k8 in range(8):
                        et = half * 8 + k8
                        tr(P, ps[pi], psb[pi][:, k8 * 128:(k8 + 1) * 128], og, og[:, et * 128:(et + 1) * 128], ident, ident[:, :])
                    cp(P, "act" if half == 0 else "dve", ogT, ogT[:, half * 8:half * 8 + 8, :], ps[pi],
                       psb[pi][:, 0:1024].rearrange("p (a b) -> p a b", b=128))
                dma(P, "pool", ogTs_t, ogT_view[:, :, tc], ogT, ogT[:, :, :])

        load(0)
        att_stage(0)
        prev = None
        for i in range(len(items)):
            t, h = items[i]
            if h == 0 and t + 1 < NT:
                load(t + 1)
            if i + 1 < len(items):
                att_stage(i + 1)
            if prev is not None:
                epi_a(*prev)
            po = main_stage(i)
            if prev is not None:
                epi_stage(*prev)
            prev = (i, po)
        epi_a(*prev)
        epi_stage(*prev)
        P.flush()
        P.end_scope()
        P.end_scope()

    def phase_linpost(inT_dram, inT_t, W2d, g1d, xold_t, xnew_t):
        P.scope()
        gb = load_gb(g1d)
        inT = P.sbT([128, KT, S], BF16, "inT")
        src = inT_dram.ap().rearrange("(kt p) s -> p kt s", p=128)
        inTq = [P.T(inT[:, :, q * 512:(q + 1) * 512]) for q in range(4)]
        for q in range(4):
            dma(P, "sp", inTq[q], inTq[q][:, :, :], inT_t, src[:, :, q * 512:(q + 1) * 512])
        slabs = [P.sbT([128, KT, 512], BF16, "w") for _ in range(4)]
        for cc in range(4):
            wslab(W2d, cc * 512, 512, slabs[cc], KT)
        xr = Rot([P.sbT([128, D], F32, "xo") for _ in range(2)])
        tmpr = Rot([P.sbT([128, D], F32, "tmp") for _ in range(2)])
        junk = P.sbT([128, 512], BF16, "junk")
        stats = stat_tiles(NT, 8)
        for t in range(NT):
            stat = stats[t]
            base = (t % 2) * 4
            xo = xr.next()
            dma(P, "sp", xo, xo[:, :], xold_t[t], xold_t[t][:, :])
            iq = inTq[t // 4]
            for cc in range(4):
                pt = ps[base + cc]
                for kt in range(KT):
                    mm(P, pt, pt[:, :], iq, inT[:, kt, t * 128:(t + 1) * 128], slabs[cc], slabs[cc][:, kt, :], kt == 0, kt == KT - 1)
                act(P, junk, junk[:, :], pt, pt[:, :], AF.Square, accum=(stat, stat[:, cc:cc + 1]))
            o = 0
            tt(P, "pool", stat, stat[:, o + 4:o + 5], stat, stat[:, o:o + 1], stat, stat[:, o + 1:o + 2], ALU.add)
            tt(P, "pool", stat, stat[:, o + 5:o + 6], stat, stat[:, o + 2:o + 3], stat, stat[:, o + 3:o + 4], ALU.add)
            tt(P, "pool", stat, stat[:, o + 4:o + 5], stat, stat[:, o + 4:o + 5], stat, stat[:, o + 5:o + 6], ALU.add)
            rstd_ops(P, stat, stat[:, o + 4:o + 5], stat[:, o + 6:o + 7], stat[:, o + 7:o + 8], D, nh, nh[:, 0:1])
            tmp = tmpr.next()
            for cc in range(4):
                pt = ps[base + cc]
                cs = slice(cc * 512, (cc + 1) * 512)
                stt(P, tmp, tmp[:, cs], pt, pt[:, :], stat[:, o + 7:o + 8], gb, gb[:, cs], ALU.mult, ALU.mult, extra_reads=[stat])
            tt(P, "pool", tmp, tmp[:, :], tmp, tmp[:, :], xo, xo[:, :], ALU.add)
            dma(P, "pool", xnew_t[t], xnew_t[t][:, :], tmp, tmp[:, :])
        P.flush()
        P.end_scope()

    def phase_ffn(layer, xold_t, xnew_t, do_d4=True):
        Wgu = ffn_w_gate_up[layer, :, :]
        Wd = ffn_w_down[layer, :, :]
        TB = 1024
        P.scope()
        for blk in range(S // TB):
            P.scope()
            hid = P.sbT([128, FT, TB], BF16, "hid")
            hidc = [P.T(hid[:, :, c * 512:(c + 1) * 512]) for c in range(2)]
            P.scope()
            gb = load_gb(pre_ffn_g[layer, :])
            hT = P.sb([128, KT, TB], BF16, "hT")
            hTc = chunk_tiles(hT, TB)
            xpool = Rot([P.sbT([128, D], F32, "xt") for _ in range(2)])
            xspool = Rot([P.sbT([128, D], BF16, "xs") for _ in range(3)])
            junk = P.sbT([128, D], BF16, "junk")
            prenorm_T(xold_t[blk * 8:(blk + 1) * 8], gb, hT, hTc, 8, xpool, xspool, junk, stat_tiles(8), Rot([0, 1]))
            SW = 256
            slabG = Rot([P.sbT([128, KT, SW], BF16, "sg") for _ in range(2)])
            slabU = Rot([P.sbT([128, KT, SW], BF16, "su") for _ in range(2)])
            sgr = Rot([P.sbT([128, 512], F32, "sil") for _ in range(3)])
            psg = Rot([2, 3, 4])
            psu = Rot([5, 6, 7])
            for s_ in range(DFF // SW):
                sg_, su_ = slabG.next(), slabU.next()
                wslab(Wgu, s_ * SW, SW, sg_, KT)
                wslab(Wgu, DFF + s_ * SW, SW, su_, KT)
                for f in range(SW // 128):
                    j = s_ * (SW // 128) + f
                    for c in range(2):
                        pg, pu = ps[psg.next()], ps[psu.next()]
                        for kt in range(KT):
                            mm(P, pg, pg[:, :], sg_, sg_[:, kt, f * 128:(f + 1) * 128], hTc[c], hT[:, kt, c * 512:(c + 1) * 512],
                               kt == 0, kt == KT - 1)
                        for kt in range(KT):
                            mm(P, pu, pu[:, :], su_, su_[:, kt, f * 128:(f + 1) * 128], hTc[c], hT[:, kt, c * 512:(c + 1) * 512],
                               kt == 0, kt == KT - 1)
                        sil = sgr.next()
                        act(P, sil, sil[:, :], pg, pg[:, :], AF.Silu)
                        tt(P, "dve", hidc[c], hid[:, j, c * 512:(c + 1) * 512], sil, sil[:, :], pu, pu[:, :], ALU.mult)
            P.flush()
            P.end_scope()
            P.scope()
            def mk_sd():
                h = P.sb([128, FT, 512], BF16, "sd")
                return (h, [P.T(h[:, p * 11:(p + 1) * 11, :]) for p in range(4)])
            slabD = Rot([mk_sd() for _ in range(2)])
            fevr = Rot([P.sbT([128, 512], F32, "fev") for _ in range(4)])
            junk = P.sbT([128, 512], BF16, "junk")
            psf = Rot([int(c) for c in os.environ.get("KBANKS", "01234567")] if debug else [0, 1, 2, 3, 4, 5, 6, 7])
            Wd_v = Wd.rearrange("(j p) n -> p j n", p=128)
            for cc in range(4):
                sd, sdp = slabD.next()
                for p in range(4):
                    dma(P, "pool", sdp[p], sd[:, p * 11:(p + 1) * 11, :], WT, Wd_v[:, p * 11:(p + 1) * 11, cc * 512:(cc + 1) * 512])
                for tl in range(8):
                    t = blk * 8 + tl
                    pf = ps[psf.next()]
                    for j in range(FT if "mm" not in SKIP else 0):
                        mm(P, pf, pf[:, :], hidc[tl // 4], hid[:, j, tl * 128:(tl + 1) * 128], sdp[j // 11], sd[:, j, :], j == 0, j == FT - 1)
                    st = fstats[t]
                    if "sq" not in SKIP:
                        act(P, junk, junk[:, :], pf, pf[:, :], AF.Square, accum=(st, st[:, cc:cc + 1]))
                    fev = fevr.next()
                    if "cp" not in SKIP:
                        cp(P, "dve", fev, fev[:, :], pf, pf[:, :])
                    if "st" not in SKIP:
                        dma(P, "sp", fscr_t[t], fscr_t[t][:, cc * 512:(cc + 1) * 512], fev, fev[:, :])
            P.flush()
            P.end_scope()
            P.end_scope()
        if not do_d4:
            P.end_scope()
            return
        P.scope()
        gb = load_gb(post_ffn_g[layer, :])
        xr = Rot([P.sbT([128, D], F32, "xo") for _ in range(3)])
        fr = Rot([P.sbT([128, D], F32, "fo") for _ in range(3)])
        for t in range(0 if "d4" not in SKIP else NT, NT):
            xo, fo = xr.next(), fr.next()
            dma(P, "sp", xo, xo[:, :], xold_t[t], xold_t[t][:, :])
            dma(P, "sp", fo, fo[:, :], fscr_t[t], fscr_t[t][:, :])
            stat = fstats[t]
            tt(P, "pool", stat, stat[:, 4:5], stat, stat[:, 0:1], stat, stat[:, 1:2], ALU.add)
            tt(P, "pool", stat, stat[:, 5:6], stat, stat[:, 2:3], stat, stat[:, 3:4], ALU.add)
            tt(P, "pool", stat, stat[:, 4:5], stat, stat[:, 4:5], stat, stat[:, 5:6], ALU.add)
            rstd_ops(P, stat, stat[:, 4:5], stat[:, 6:7], stat[:, 7:8], D, nh, nh[:, 0:1])
            stt(P, fo, fo[:, :], fo, fo[:, :], stat[:, 7:8], gb, gb[:, :], ALU.mult, ALU.mult, extra_reads=[stat])
            tt(P, "dve", fo, fo[:, :], fo, fo[:, :], xo, xo[:, :], ALU.add)
            dma(P, "pool", xnew_t[t], xnew_t[t][:, :], fo, fo[:, :])
        P.flush()
        P.end_scope()
        P.end_scope()

    def phase_E(fuse_ffn=True):
        for which in range(2):
            P.scope()
            gb = load_gb(kv_norm_g.ap() if which == 0 else pre_mix_g[1, :])
            hT = P.sb([128, KT, S], BF16, "hT")
            hTc = chunk_tiles(hT, S)
            xspool = Rot([P.sbT([128, D], BF16, "xs") for _ in range(3)])
            junk = P.sbT([128, D], BF16, "junk")
            if which == 0 and fuse_ffn:
                xpool = Rot([P.sbT([128, D], F32, "xt") for _ in range(3)])
                fuse = {"fpool": Rot([P.sbT([128, D], F32, "ft") for _ in range(2)]), "f": fscr_t, "stats": fstats,
                        "gpost": load_gb(post_ffn_g[0, :]), "xnew": x2_t}
                prenorm_T(x1_t, gb, hT, hTc, NT, xpool, xspool, junk, stat_tiles(NT), Rot([0, 1]), fuse=fuse)
            else:
                xpool = Rot([P.sbT([128, D], F32, "xt") for _ in range(2)])
                prenorm_T(x2_t, gb, hT, hTc, NT, xpool, xspool, junk, stat_tiles(NT), Rot([0, 1]))
            W = w_kv[:, :] if which == 0 else diff_w_q[:, :]
            dstT = kT2_t if which == 0 else qT2_t
            scale = 1.0 if which == 0 else 128.0 ** -0.5
            slabsB = Rot([P.sbT([128, KT, 256], BF16, "slabB") for _ in range(2)])
            evB = Rot([P.sbT([128, S], BF16, "evB") for _ in range(2)])
            psr = Rot([2, 3, 4, 5, 6, 7])
            for sidx in range(8):
                slab = slabsB.next()
                wslab(W, sidx * 256, 256, slab, KT)
                for n in range(2):
                    nt_idx = sidx * 2 + n
                    ev = evB.next()
                    for c in range(4):
                        pt = ps[psr.next()]
                        for kt in range(KT):
                            mm(P, pt, pt[:, :], slab, slab[:, kt, n * 128:(n + 1) * 128], hTc[c], hT[:, kt, c * 512:(c + 1) * 512],
                               kt == 0, kt == KT - 1)
                        if c % 2 == 0:
                            act(P, ev, ev[:, c * 512:(c + 1) * 512], pt, pt[:, :], AF.Copy, scale=scale)
                        else:
                            ts(P, "dve", ev, ev[:, c * 512:(c + 1) * 512], pt, pt[:, :], scale, None, ALU.mult)
                    dma(P, "sp", dstT[nt_idx], dstT[nt_idx][:, :], ev, ev[:, :])
            if which == 0:
                slabsA = Rot([P.sbT([128, KT, 512], BF16, "slabA") for _ in range(2)])
                evA = Rot([P.sbT([128, 512], BF16, "evA") for _ in range(4)])
                for ch in range(4):
                    slab = slabsA.next()
                    wslab(W, 2048 + ch * 512, 512, slab, KT)
                    for t in range(NT):
                        pt = ps[psr.next()]
                        for kt in range(KT):
                            mm(P, pt, pt[:, :], hTc[t // 4], hT[:, kt, t * 128:(t + 1) * 128], slab, slab[:, kt, :], kt == 0, kt == KT - 1)
                        ev = evA.next()
                        cp(P, "dve" if t % 2 == 0 else "act", ev, ev[:, :], pt, pt[:, :])
                        dma(P, "sp", v2_t[t], v2_t[t][:, ch * 512:(ch + 1) * 512], ev, ev[:, :])
            P.flush()
            P.end_scope()

    def phase_F():
        P.scope()
        Bh = P.sbT([128, 8, 256], F32, "Bh")
        dma(P, "sp", Bh, Bh[:, :, :], WT, bias_t[:, :, :])
        for h in range(8):
            P.add("pool", lambda e, h=h: e.affine_select(out=Bh[:, h, 0:128], in_=Bh[:, h, 0:128], pattern=[[1, 128]],
                                                         compare_op=ALU.is_ge, fill=MASKV, base=0, channel_multiplier=-1),
                  reads=[Bh], writes=[Bh])
        kTr = Rot([P.sbT([128, 2, S], BF16, "kTh") for _ in range(2)])
        qTr = Rot([P.sbT([128, 2, S], BF16, "qTh") for _ in range(2)])
        Vr = Rot([P.sbT([128, NT, 256], BF16, "Vh") for _ in range(2)])
        PTr = Rot([P.sbT([128, 512], BF16, "PT") for _ in range(6)])
        tbr = Rot([P.sbT([128, 256], F32, "tb") for _ in range(4)])
        rlr = [Rot([P.sbT([128, 512], F32, "rl") for _ in range(2)]) for _ in range(2)]
        ar = [Rot([P.sbT([128, 512], F32, "a") for _ in range(2)]) for _ in range(2)]
        t1r = Rot([P.sbT([128, 512], F32, "t1") for _ in range(4)])
        sqr = Rot([P.sbT([128, 512], BF16, "sq") for _ in range(4)])
        rsr = Rot([P.sbT([128, 512], F32, "rs") for _ in range(2)])
        oTr = Rot([P.sbT([128, 512], BF16, "oT") for _ in range(4)])
        ps_s = Rot([6, 7])
        psO = [[ps[0], ps[1]], [ps[2], ps[3]]]
        psL = [ps[4], ps[5]]
        kv_view = kT2.ap().rearrange("(n p) s -> p n s", p=128)
        q_view = qT2.ap().rearrange("(n p) s -> p n s", p=128)
        v_view = v2.ap().rearrange("(t p) e -> p t e", p=128)
        kT2_all = P.T(kT2[:, :])
        qT2_all = P.T(qT2[:, :])
        v2_all = P.T(v2[:, :])
        head_tiles = {}

        def load_head(h):
            kTh, qTh, Vh = kTr.next(), qTr.next(), Vr.next()
            dma(P, "sp", kTh, kTh[:, :, :], kT2_all, kv_view[:, 2 * h:2 * h + 2, :])
            dma(P, "sp", qTh, qTh[:, :, :], qT2_all, q_view[:, 2 * h:2 * h + 2, :])
            dma(P, "sp", Vh, Vh[:, :, :], v2_all, v_view[:, :, h * 256:(h + 1) * 256])
            head_tiles[h] = (kTh, qTh, Vh)

        steps = []
        for h in range(8):
            for qb in range(4):
                nj = 4 * qb + 4
                for j in range(nj):
                    for m in range(2):
                        steps.append((h, qb, j, m, nj))

        def qk_exp(s_):
            h, qb, j, m, nj = steps[s_]
            kTh, qTh, Vh = head_tiles[h]
            c0 = max(0, (j - 4 * qb) * 128)
            if j >= 4 * qb:
                nb = 2 if (j - 4 * qb) < 3 else 1
                b0 = 0
            elif j == 4 * qb - 1:
                nb, b0 = 1, 128
            else:
                nb, b0 = 0, 0
            pS = ps[ps_s.next()]
            mm(P, pS, pS[:, c0:512], kTh, kTh[:, m, j * 128:(j + 1) * 128], qTh, qTh[:, m, qb * 512 + c0:(qb + 1) * 512], True, True)
            PT = PTr.next()
            c1 = c0 + nb * 128
            if nb > 0:
                tb = tbr.next()
                tt(P, "dve", tb, tb[:, 0:nb * 128], pS, pS[:, c0:c1], Bh, Bh[:, h, b0:b0 + nb * 128], ALU.add)
            if c1 < 512:
                act(P, PT, PT[:, c1:512], pS, pS[:, c1:512], AF.Exp, bias=ctab_b[:, h:h + 1], extra_reads=[ctab_b])
            if nb > 0:
                act(P, PT, PT[:, c0:c1], tb, tb[:, 0:nb * 128], AF.Exp)
            return PT, c0

        def pv(s_, PT, c0):
            h, qb, j, m, nj = steps[s_]
            kTh, qTh, Vh = head_tiles[h]
            for e_ in range(2):
                po = psO[m][e_]
                mm(P, po, po[:, c0:512], Vh, Vh[:, j, e_ * 128:(e_ + 1) * 128], PT, PT[:, c0:512], j == 0, j == nj - 1)
            mm(P, psL[m], psL[m][:, c0:512], ones_b, ones_b[:, :], PT, PT[:, c0:512], j == 0, j == nj - 1)

        deferred = []

        def finalize(h, qb, s_now):
            rl = [rlr[0].next(), rlr[1].next()]
            for m in range(2):
                P.add("dve", lambda e, o=rl[m], i=psL[m]: e.reciprocal(out=o[:, :], in_=i[:, :]), reads=[psL[m]], writes=[rl[m]])
            a = [ar[0].next(), ar[1].next()]
            t1 = [t1r.next(), t1r.next()]
            for e_ in range(2):
                stt(P, t1[e_], t1[e_][:, :], psO[1][e_], psO[1][e_][:, :], neglam[:, 0:1], rl[1], rl[1][:, :], ALU.mult, ALU.mult,
                    extra_reads=[neglam])
                tt(P, "dve", a[e_], a[e_][:, :], psO[0][e_], psO[0][e_][:, :], rl[0], rl[0][:, :], ALU.mult)
            sq = [sqr.next(), sqr.next()]
            rs = rsr.next()

            def part2a():
                for e_ in range(2):
                    tt(P, "pool", a[e_], a[e_][:, :], a[e_], a[e_][:, :], t1[e_], t1[e_][:, :], ALU.add)
                    act(P, sq[e_], sq[e_][:, :], a[e_], a[e_][:, :], AF.Square)

            def part2b():
                pS = ps[ps_s.next()]
                for e_ in range(2):
                    mm(P, pS, pS[:, :], ones_b, ones_b[:, :], sq[e_], sq[e_][:, :], e_ == 0, e_ == 1)
                act(P, rs, rs[:, :], pS, pS[:, :], AF.Ln, scale=1.0 / 256, bias=EPS)
                act(P, rs, rs[:, :], rs, rs[:, :], AF.Exp, scale=-0.5)

            def part2c():
                for e_ in range(2):
                    oT = oTr.next()
                    stt(P, oT, oT[:, :], a[e_], a[e_][:, :], gsub[:, e_:e_ + 1], rs, rs[:, :], ALU.mult, ALU.mult, extra_reads=[gsub])
                    r0 = (2 * h + e_) * 128
                    dma(P, "pool", oTs_t, oT_s[r0:r0 + 128, qb * 512:(qb + 1) * 512], oT, oT[:, :])

            deferred.append((s_now + 1, part2a))
            deferred.append((s_now + 2, part2b))
            deferred.append((s_now + 4, part2c))

        load_head(0)
        cur = qk_exp(0)
        for s_ in range(len(steps)):
            h, qb, j, m, nj = steps[s_]
            if qb == 0 and j == 0 and m == 0 and h + 1 < 8:
                load_head(h + 1)
            nxt = qk_exp(s_ + 1) if s_ + 1 < len(steps) else None
            pv(s_, *cur)
            if j == nj - 1 and m == 1:
                finalize(h, qb, s_)
            while deferred and deferred[0][0] <= s_:
                deferred.pop(0)[1]()
            cur = nxt
        while deferred:
            deferred.pop(0)[1]()
        P.flush()
        P.end_scope()

    import os
    stop = int(os.environ.get("KSTOP", "99")) if debug else 99
    steps = [
        phase_A,
        phase_B,
        lambda: phase_linpost(ogT_s, ogTs_t, gla_w_out[:, :], post_mix_g[0, :], xin_t, x1_t),
        lambda: phase_ffn(0, x1_t, x2_t, do_d4=False),
        phase_E,
        phase_F,
        lambda: phase_linpost(oT_s, oTs_t, diff_w_out[:, :], post_mix_g[1, :], x2_t, x3_t),
        lambda: phase_ffn(1, x3_t, y_t),
    ]
    for i, st_ in enumerate(steps):
        if i < stop:
            st_()
    print("instructions:", P.n_inst)
    return nc


def _t5_bucket_np(dist):
    n = np.maximum(dist, 0)
    nf = np.maximum(n, 1).astype(np.float32)
    large = 16 + (np.log(nf / 16) / np.float32(math.log(128 / 16)) * 16).astype(np.int32)
    large = np.minimum(large, 31)
    return np.where(n < 16, n, large)


def prep_inputs(inputs):
    f = lambda a: np.ascontiguousarray(np.asarray(a, dtype=np.float32))
    table = f(inputs["rel_bias_table"])
    k = np.arange(128)[:, None]
    c = np.arange(256)[None, :]
    idx = _t5_bucket_np(c - k)
    bias_t = np.ascontiguousarray(np.transpose(table[idx], (0, 2, 1)))
    shared = {
        "bias_t": bias_t,
        "ctab": f(table[31, :]),
        "kv_norm_g": f(inputs["kv_norm_g"]),
        "w_kv": f(inputs["w_kv"]),
        "gla_w_in": f(inputs["gla_w_in"][0]),
        "gla_w_fgate": f(inputs["gla_w_fgate"][0]),
        "gla_b_fgate": f(inputs["gla_b_fgate"][0]),
        "gla_norm_g": f(inputs["gla_norm_g"][0]),
        "gla_w_out": f(inputs["gla_w_out"][0]),
        "diff_w_q": f(inputs["diff_w_q"][0]),
        "lam4": f(np.stack([inputs["diff_lam_q1"][0], inputs["diff_lam_k1"][0], inputs["diff_lam_q2"][0], inputs["diff_lam_k2"][0]])),
        "diff_subln_g": f(inputs["diff_subln_g"][0]),
        "diff_w_out": f(inputs["diff_w_out"][0]),
        "pre_mix_g": f(inputs["pre_mix_g"]),
        "post_mix_g": f(inputs["post_mix_g"]),
        "pre_ffn_g": f(inputs["pre_ffn_g"]),
        "post_ffn_g": f(inputs["post_ffn_g"]),
        "ffn_w_gate_up": f(inputs["ffn_w_gate_up"]),
        "ffn_w_down": f(inputs["ffn_w_down"]),
    }
    x = f(inputs["x"])
    return [dict(shared, x=x[b]) for b in range(x.shape[0])]


_NC_CACHE = {}


def kernel(**inputs):
    in_maps = prep_inputs(inputs)
    if "nc" not in _NC_CACHE:
        _NC_CACHE["nc"] = build_nc()
    nc = _NC_CACHE["nc"]
    res = run_bass_kernel_spmd(nc, in_maps, core_ids=list(range(8)))
    return np.stack([np.asarray(r["y"], dtype=np.float32) for r in res.results], axis=0)
```

```python
import contextlib
import math
import numpy as np
import concourse.bass as bass
import concourse.mybir as mybir
from concourse.bass_utils import run_bass_kernel_spmd

F32 = mybir.dt.float32
BF16 = mybir.dt.bfloat16
AF = mybir.ActivationFunctionType
ALU = mybir.AluOpType

S = 2048
D = 2048
NT = S // 128
KT = D // 128
DFF = 5632
FT = DFF // 128
GLA_IN = 6160
EPS = 1e-6
LAMBDA_INIT = 0.8 - 0.6 * math.exp(-0.3 * 1)
MASKV = -30000.0


class Tl:
    __slots__ = ("ap", "w", "r", "dsem", "psum")

    def __init__(self, ap):
        self.ap = ap
        self.w = None
        self.r = []
        self.dsem = None
        self.psum = "PSum" in type(ap.tensor).__name__

    def __getitem__(self, k):
        return self.ap[k]


class Op:
    __slots__ = ("q", "fn", "deps", "seq", "need", "dma", "dsem", "dval")

    def __init__(self, q, fn, dma):
        self.q = q
        self.fn = fn
        self.deps = {}
        self.seq = None
        self.need = False
        self.dma = dma
        self.dsem = None
        self.dval = None


QUEUES = ("pe", "act", "dve", "pool", "sp")
import os as _os
STRICT = _os.environ.get("KSTRICT", "0") == "1"


class Prog:
    def __init__(self, nc):
        self.nc = nc
        self.ops = {q: [] for q in QUEUES}
        self.qsem = {q: nc.alloc_semaphore("q_" + q) for q in QUEUES}
        self.qcount = {q: 0 for q in QUEUES}
        self.seen = {q: {} for q in QUEUES}
        self.free_dsems = {"hw": [nc.alloc_semaphore("dh%d" % i) for i in range(45)],
                           "sw": [nc.alloc_semaphore("ds%d" % i) for i in range(45)]}
        self.semval = {}
        self.tiles = []
        self.phase_dsems = []
        self.stacks = []
        self.n_inst = 0
        self.uid = 0

    def scope(self):
        st = contextlib.ExitStack()
        self.stacks.append(st)
        return st

    def end_scope(self):
        self.stacks.pop().close()

    def sb(self, shape, dtype, name=None):
        self.uid += 1
        h = self.stacks[-1].enter_context(self.nc.sbuf_tensor("%s_%d" % (name or "t", self.uid), list(shape), dtype))
        return h

    def T(self, ap):
        t = Tl(ap)
        self.tiles.append(t)
        return t

    def sbT(self, shape, dtype, name=None):
        h = self.sb(shape, dtype, name)
        return self.T(h[tuple(slice(None) for _ in shape)])

    def add(self, q, fn, reads=(), writes=(), dma=False):
        op = Op(q, fn, dma)
        for t in reads:
            if t.w is not None:
                op.deps[t.w] = True
            if t.psum:
                for r in t.r:
                    if r.q != q:
                        op.deps.setdefault(r, False)
        for t in writes:
            if t.w is not None:
                op.deps.setdefault(t.w, False)
            for r in t.r:
                op.deps.setdefault(r, False)
        if dma:
            st = None
            for t in list(writes) + list(reads):
                if t.dsem is not None or _is_sbuf(t):
                    st = t
                    break
            assert st is not None
            kind = "sw" if q == "pool" else "hw"
            if st.dsem is None:
                st.dsem = {}
            if kind not in st.dsem:
                sem_ = self.free_dsems[kind].pop()
                st.dsem[kind] = sem_
                self.phase_dsems.append((st, kind, sem_))
                self.semval.setdefault(sem_, 0)
            sem_ = st.dsem[kind]
            self.semval[sem_] += 16
            op.dsem = sem_
            op.dval = self.semval[sem_]
        for t in reads:
            t.r.append(op)
        for t in writes:
            t.w = op
            t.r = []
        self.ops[q].append(op)
        return op

    def flush(self):
        nc = self.nc
        drain = [(sem, self.semval[sem]) for (_, _, sem) in self.phase_dsems]
        for q in QUEUES:
            for op in self.ops[q]:
                for d, raw in op.deps.items():
                    if d.dma:
                        continue
                    if d.q == op.q and not op.dma and not raw and not (STRICT and op.q != "pe"):
                        continue
                    d.need = True
        for q in QUEUES:
            for op in self.ops[q]:
                if op.need and not op.dma:
                    self.qcount[q] += 1
                    op.seq = self.qcount[q]
        engs = {"pe": "tensor", "act": "scalar", "dve": "vector", "pool": "gpsimd", "sp": "sync"}

        def emit(q, eng):
            seen = self.seen[q]
            for op in self.ops[q]:
                w = {}
                for d, raw in op.deps.items():
                    if d.dma:
                        sem, val = d.dsem, d.dval
                    else:
                        if d.q == op.q and not op.dma and not raw and not (STRICT and op.q != "pe"):
                            continue
                        sem, val = self.qsem[d.q], d.seq
                    if w.get(sem, 0) < val:
                        w[sem] = val
                for sem, val in w.items():
                    if seen.get(sem, 0) < val:
                        eng.wait_ge(sem, val)
                        seen[sem] = val
                ins = op.fn(eng)
                self.n_inst += 1
                if op.dma:
                    ins.then_inc(op.dsem, 16)
                elif op.need:
                    ins.then_inc(self.qsem[q], 1)
            if q == "sp":
                for sem, val in drain:
                    if seen.get(sem, 0) < val:
                        eng.wait_ge(sem, val)
                        seen[sem] = val

        with nc.Block() as blk:
            for q in QUEUES:
                if self.ops[q] or q == "sp":
                    getattr(blk, engs[q])(lambda eng, q=q: emit(q, eng))
        for q in QUEUES:
            self.ops[q] = []
        for t, kind, sem in self.phase_dsems:
            t.dsem = None
            self.free_dsems[kind].append(sem)
        self.phase_dsems = []
        for t in self.tiles:
            t.w = None
            t.r = []
        self.tiles = [t for t in self.tiles if not _is_sbuf(t) or True]


def _is_sbuf(t):
    return "SBTensor" in type(t.ap.tensor).__name__


def dma(P, q, out_t, out_ap, in_t, in_ap):
    return P.add(q, lambda e: e.dma_start(out=out_ap, in_=in_ap), reads=[in_t], writes=[out_t], dma=True)


def mm(P, ps_t, out_ap, lhsT_t, lhsT_ap, rhs_t, rhs_ap, start, stop):
    return P.add("pe", lambda e: e.matmul(out_ap, lhsT=lhsT_ap, rhs=rhs_ap, start=start, stop=stop),
                 reads=[lhsT_t, rhs_t], writes=[ps_t])


def tr(P, ps_t, out_ap, in_t, in_ap, id_t, id_ap):
    return P.add("pe", lambda e: e.transpose(out=out_ap, in_=in_ap, identity=id_ap), reads=[in_t, id_t], writes=[ps_t])


def act(P, out_t, out_ap, in_t, in_ap, func, bias=None, scale=None, accum=None, extra_reads=(), q="act"):
    kw = {}
    if bias is not None:
        kw["bias"] = bias
    if scale is not None:
        kw["scale"] = scale
    wr = [out_t]
    if accum is not None:
        kw["accum_out"] = accum[1]
        wr.append(accum[0])
    return P.add("act", lambda e: e.activation(out=out_ap, in_=in_ap, func=func, **kw),
                 reads=[in_t] + list(extra_reads), writes=wr)


def tt(P, q, out_t, out_ap, a_t, a_ap, b_t, b_ap, op):
    return P.add(q, lambda e: e.tensor_tensor(out=out_ap, in0=a_ap, in1=b_ap, op=op), reads=[a_t, b_t], writes=[out_t])


def ts(P, q, out_t, out_ap, a_t, a_ap, s1, s2, op0, op1=None, extra_reads=()):
    if op1 is None:
        return P.add(q, lambda e: e.tensor_scalar(out=out_ap, in0=a_ap, scalar1=s1, scalar2=None, op0=op0),
                     reads=[a_t] + list(extra_reads), writes=[out_t])
    return P.add(q, lambda e: e.tensor_scalar(out=out_ap, in0=a_ap, scalar1=s1, scalar2=s2, op0=op0, op1=op1),
                 reads=[a_t] + list(extra_reads), writes=[out_t])


def stt(P, out_t, out_ap, a_t, a_ap, scalar, b_t, b_ap, op0, op1, extra_reads=()):
    return P.add("dve", lambda e: e.scalar_tensor_tensor(out=out_ap, in0=a_ap, scalar=scalar, in1=b_ap, op0=op0, op1=op1),
                 reads=[a_t, b_t] + list(extra_reads), writes=[out_t])


def cp(P, q, out_t, out_ap, in_t, in_ap):
    if q == "act":
        return P.add("act", lambda e: e.activation(out=out_ap, in_=in_ap, func=AF.Copy), reads=[in_t], writes=[out_t])
    return P.add(q, lambda e: e.tensor_copy(out=out_ap, in_=in_ap), reads=[in_t], writes=[out_t])


def memset(P, q, t, ap, val):
    return P.add(q, lambda e: e.memset(ap, val), writes=[t])


def rstd_ops(P, st_t, ss_ap, tmp_ap, out_ap, n, nh_t, nh_ap):
    ts(P, "pool", st_t, tmp_ap, st_t, ss_ap, 1.0 / n, EPS, ALU.mult, ALU.add)
    P.add("pool", lambda e: e.tensor_tensor(out=out_ap, in0=tmp_ap, in1=nh_ap, op=ALU.pow), reads=[st_t, nh_t], writes=[st_t])


class Rot:
    def __init__(self, items):
        self.items = items
        self.i = 0

    def next(self):
        t = self.items[self.i % len(self.items)]
        self.i += 1
        return t


def build_nc(debug=False):
    import os
    SKIP = set(os.environ.get("KSKIP", "").split(",")) if debug else set()
    nc = bass.Bass("TRN2", target_bir_lowering=False)
    P = Prog(nc)

    def din(name, shape, dt=F32):
        return nc.dram_tensor(name, list(shape), dt, kind="ExternalInput")

    def dscr(name, shape, dt=F32, out=False):
        return nc.dram_tensor(name, list(shape), dt, kind=("ExternalOutput" if out else "Internal"))

    x_in = din("x", [S, D])
    bias_t = din("bias_t", [128, 8, 256])
    ctab = din("ctab", [8])
    kv_norm_g = din("kv_norm_g", [D])
    w_kv = din("w_kv", [D, 2 * D])
    gla_w_in = din("gla_w_in", [D, GLA_IN])
    gla_w_fgate = din("gla_w_fgate", [16, 1024])
    gla_b_fgate = din("gla_b_fgate", [1024])
    gla_norm_g = din("gla_norm_g", [512])
    gla_w_out = din("gla_w_out", [D, D])
    diff_w_q = din("diff_w_q", [D, D])
    lam4 = din("lam4", [4, 128])
    diff_subln_g = din("diff_subln_g", [256])
    diff_w_out = din("diff_w_out", [D, D])
    pre_mix_g = din("pre_mix_g", [2, D])
    post_mix_g = din("post_mix_g", [2, D])
    pre_ffn_g = din("pre_ffn_g", [2, D])
    post_ffn_g = din("post_ffn_g", [2, D])
    ffn_w_gate_up = din("ffn_w_gate_up", [2, D, 2 * DFF])
    ffn_w_down = din("ffn_w_down", [2, DFF, D])

    x1 = dscr("x1", [S, D], out=debug)
    x2 = dscr("x2", [S, D], out=debug)
    x3 = dscr("x3", [S, D], out=debug)
    y = dscr("y", [S, D], out=True)
    fscr = dscr("fscr", [S, D])
    qT_s = fscr[0:1024, :]
    kT_s = fscr[1024:2048, :]
    v_s = dscr("v_s", [S, D], BF16)
    sr_s = dscr("sr_s", [S, D], BF16)
    glr_s = dscr("glr_s", [16, S])
    ogT_s = dscr("ogT_s", [D, S], BF16)
    kT2 = dscr("kT2", [D, S], BF16)
    qT2 = dscr("qT2", [D, S], BF16)
    v2 = dscr("v2", [S, D], BF16)
    oT_s = dscr("oT_s", [D, S], BF16)

    def rowtiles(h):
        return [P.T(h[i * 128:(i + 1) * 128, :]) for i in range(h.shape[0] // 128)]

    xin_t = rowtiles(x_in)
    x1_t, x2_t, x3_t, y_t = rowtiles(x1), rowtiles(x2), rowtiles(x3), rowtiles(y)
    WT = P.T(gla_w_in[:, :])
    qTs_t, kTs_t = rowtiles(qT_s), rowtiles(kT_s)
    vs_t, srs_t = rowtiles(v_s), rowtiles(sr_s)
    glrs_t = P.T(glr_s[:, :])
    ogTs_t = P.T(ogT_s[:, :])
    kT2_t, qT2_t = rowtiles(kT2), rowtiles(qT2)
    v2_t = rowtiles(v2)
    oTs_t = P.T(oT_s[:, :])
    fscr_t = rowtiles(fscr)

    g0 = P.scope()
    ps = [P.T(nc.alloc_psum_tensor("ps%d" % i, [128, 512], F32)[:, :]) for i in range(8)]
    psb = [t.ap.tensor.bitcast(BF16) for t in ps]
    ident = P.sbT([128, 128], BF16, "ident")
    identf = P.sbT([128, 128], F32, "identf")
    ones_b = P.sbT([128, 128], BF16, "ones_b")
    ones_f = P.sbT([128, 128], F32, "ones_f")
    maskT = P.sbT([128, 128], F32, "maskT")
    nh = P.sbT([128, 512], F32, "nh")
    neglam = P.sbT([128, 4], F32, "neglam")
    gsub = P.sbT([128, 2], F32, "gsub")
    lamt = P.sbT([128, 4], F32, "lamt")
    ctab_b = P.sbT([128, 8], F32, "ctab_b")
    _fs_h = P.sb([128, NT * 8], F32, "fstats")
    fstats = [P.T(_fs_h[:, i * 8:(i + 1) * 8]) for i in range(NT)]

    memset(P, "pool", identf, identf[:, :], 1.0)
    P.add("pool", lambda e: e.affine_select(out=identf[:, :], in_=identf[:, :], pattern=[[1, 128]], compare_op=ALU.is_equal,
                                             fill=0.0, base=0, channel_multiplier=-1), reads=[identf], writes=[identf])
    cp(P, "pool", ident, ident[:, :], identf, identf[:, :])
    memset(P, "pool", ones_b, ones_b[:, :], 1.0)
    memset(P, "pool", ones_f, ones_f[:, :], 1.0)
    memset(P, "pool", nh, nh[:, :], -0.5)
    memset(P, "pool", maskT, maskT[:, :], 1.0)
    P.add("pool", lambda e: e.affine_select(out=maskT[:, :], in_=maskT[:, :], pattern=[[1, 128]], compare_op=ALU.is_ge,
                                             fill=0.0, base=0, channel_multiplier=-1), reads=[maskT], writes=[maskT])
    for i in range(4):
        dma(P, "sp", lamt, lamt[:, i:i + 1], WT, lam4[i, :].rearrange("(p o) -> p o", o=1))
    for i in range(2):
        dma(P, "sp", gsub, gsub[:, i:i + 1], WT, diff_subln_g[i * 128:(i + 1) * 128].rearrange("(p o) -> p o", o=1))
    dma(P, "sp", ctab_b, ctab_b[:, :], WT, bass.AP(ctab, 0, [[0, 128], [1, 8]]))
    tt(P, "dve", neglam, neglam[:, 0:1], lamt, lamt[:, 0:1], lamt, lamt[:, 1:2], ALU.mult)
    tt(P, "dve", neglam, neglam[:, 1:2], lamt, lamt[:, 2:3], lamt, lamt[:, 3:4], ALU.mult)
    mm(P, ps[0], ps[0][:, 0:2], ones_f, ones_f[:, :], neglam, neglam[:, 0:2], True, True)
    act(P, neglam, neglam[:, 2:4], ps[0], ps[0][:, 0:2], AF.Exp)
    tt(P, "dve", neglam, neglam[:, 0:1], neglam, neglam[:, 3:4], neglam, neglam[:, 2:3], ALU.subtract)
    ts(P, "dve", neglam, neglam[:, 0:1], neglam, neglam[:, 0:1], -LAMBDA_INIT, None, ALU.add)
    ts(P, "dve", gsub, gsub[:, 0:2], gsub, gsub[:, 0:2], 1.0 - LAMBDA_INIT, None, ALU.mult)
    P.flush()

    def load_gb(g_ap_1d):
        gb = P.sbT([128, D], F32, "gb")
        n = g_ap_1d.shape[0]
        dma(P, "sp", gb, gb[:, 0:n], WT, bass.AP(g_ap_1d.tensor, g_ap_1d.offset, [[0, 128], [1, n]]))
        return gb

    def stat_tiles(n, w=4):
        h = P.sb([128, n * w], F32, "stat")
        return [P.T(h[:, i * w:(i + 1) * w]) for i in range(n)]

    def chunk_tiles(h, ncols):
        return [P.T(h[:, :, c * 512:(c + 1) * 512]) for c in range(ncols // 512)]

    def prenorm_T(xtiles, gb, hT, hTc, nt, xpool, xspool, junk, stats, psr, fuse=None):
        xs_of = {}

        def stage_a(li):
            xt = xpool.next()
            st = stats[li]
            dma(P, "sp", xt, xt[:, :], xtiles[li], xtiles[li][:, :])
            if fuse is not None:
                ft = fuse["fpool"].next()
                dma(P, "sp", ft, ft[:, :], fuse["f"][li], fuse["f"][li][:, :])
                fs = fuse["stats"][li]
                gp = fuse["gpost"]
                tt(P, "pool", fs, fs[:, 4:5], fs, fs[:, 0:1], fs, fs[:, 1:2], ALU.add)
                tt(P, "pool", fs, fs[:, 5:6], fs, fs[:, 2:3], fs, fs[:, 3:4], ALU.add)
                tt(P, "pool", fs, fs[:, 4:5], fs, fs[:, 4:5], fs, fs[:, 5:6], ALU.add)
                rstd_ops(P, fs, fs[:, 4:5], fs[:, 6:7], fs[:, 7:8], D, nh, nh[:, 0:1])
                stt(P, ft, ft[:, :], ft, ft[:, :], fs[:, 7:8], gp, gp[:, :], ALU.mult, ALU.mult, extra_reads=[fs])
                tt(P, "pool", xt, xt[:, :], xt, xt[:, :], ft, ft[:, :], ALU.add)
                dma(P, "pool", fuse["xnew"][li], fuse["xnew"][li][:, :], xt, xt[:, :])
            act(P, junk, junk[:, :], xt, xt[:, :], AF.Square, accum=(st, st[:, 0:1]))
            rstd_ops(P, st, st[:, 0:1], st[:, 1:2], st[:, 2:3], D, nh, nh[:, 0:1])
            xs = xspool.next()
            stt(P, xs, xs[:, :], xt, xt[:, :], st[:, 2:3], gb, gb[:, :], ALU.mult, ALU.mult, extra_reads=[st])
            xs_of[li] = xs

        def stage_b(li):
            xs = xs_of.pop(li)
            for half in range(2):
                pi = psr.next()
                pt, pb = ps[pi], psb[pi]
                for k in range(8):
                    kt = half * 8 + k
                    tr(P, pt, pb[:, k * 128:(k + 1) * 128], xs, xs[:, kt * 128:(kt + 1) * 128], ident, ident[:, :])
                q = "act" if half == 0 else "dve"
                cp(P, q, hTc[li // 4], hT[:, half * 8:half * 8 + 8, li * 128:(li + 1) * 128], pt,
                   pb[:, 0:1024].rearrange("p (a b) -> p a b", b=128))

        stage_a(0)
        for li in range(nt):
            if li + 1 < nt:
                stage_a(li + 1)
            stage_b(li)

    def wslab(W2d, c0, ncols, slab, kts):
        src = W2d.rearrange("(kt p) n -> p kt n", p=128)[:, :, c0:c0 + ncols]
        dma(P, "pool", slab, slab[:, 0:kts, 0:ncols], WT, src)

    def phase_A():
        P.scope()
        w_in = gla_w_in[:, :]
        gb = load_gb(pre_mix_g[0, :])
        hT = P.sb([128, KT, S], BF16, "hT")
        hTc = chunk_tiles(hT, S)
        xpool = Rot([P.sbT([128, D], F32, "xt") for _ in range(2)])
        xspool = Rot([P.sbT([128, D], BF16, "xs") for _ in range(3)])
        junk = P.sbT([128, D], BF16, "junk")
        prenorm_T(xin_t, gb, hT, hTc, NT, xpool, xspool, junk, stat_tiles(NT), Rot([0, 1]))
        slabsB = Rot([P.sbT([128, KT, 256], BF16, "slabB") for _ in range(2)])
        evB = Rot([P.sbT([128, S], F32, "evB") for _ in range(2)])
        psr = Rot([2, 3, 4, 5, 6, 7])
        for sidx in range(8):
            slab = slabsB.next()
            wslab(w_in, sidx * 256, 256, slab, KT)
            for n in range(2):
                nt_idx = sidx * 2 + n
                ev = evB.next()
                for c in range(4):
                    pt = ps[psr.next()]
                    for kt in range(KT):
                        mm(P, pt, pt[:, :], slab, slab[:, kt, n * 128:(n + 1) * 128], hTc[c], hT[:, kt, c * 512:(c + 1) * 512],
                           kt == 0, kt == KT - 1)
                    cp(P, "act" if c % 2 == 0 else "dve", ev, ev[:, c * 512:(c + 1) * 512], pt, pt[:, :])
                dst = qTs_t[nt_idx] if nt_idx < 8 else kTs_t[nt_idx - 8]
                dma(P, "sp", dst, dst[:, :], ev, ev[:, :])
        slab = slabsB.next()
        wslab(w_in, 6144, 16, slab, KT)
        ev = evB.next()
        for c in range(4):
            pt = ps[psr.next()]
            for kt in range(KT):
                mm(P, pt, pt[0:16, :], slab, slab[:, kt, 0:16], hTc[c], hT[:, kt, c * 512:(c + 1) * 512], kt == 0, kt == KT - 1)
            cp(P, "act", ev, ev[0:16, c * 512:(c + 1) * 512], pt, pt[0:16, :])
        dma(P, "sp", glrs_t, glrs_t[:, :], ev, ev[0:16, :])
        slabsA = Rot([P.sbT([128, KT, 512], BF16, "slabA") for _ in range(2)])
        evA = Rot([P.sbT([128, 512], BF16, "evA") for _ in range(4)])
        for ch in range(8):
            slab = slabsA.next()
            wslab(w_in, 2048 + ch * 512, 512, slab, KT)
            is_v = ch < 4
            for t in range(NT):
                pt = ps[psr.next()]
                for kt in range(KT):
                    mm(P, pt, pt[:, :], hTc[t // 4], hT[:, kt, t * 128:(t + 1) * 128], slab, slab[:, kt, :], kt == 0, kt == KT - 1)
                ev = evA.next()
                if is_v:
                    cp(P, "dve", ev, ev[:, :], pt, pt[:, :])
                    dst = vs_t[t]
                    dma(P, "sp", dst, dst[:, ch * 512:(ch + 1) * 512], ev, ev[:, :])
                else:
                    act(P, ev, ev[:, :], pt, pt[:, :], AF.Silu)
                    dst = srs_t[t]
                    dma(P, "sp", dst, dst[:, (ch - 4) * 512:(ch - 3) * 512], ev, ev[:, :])
        P.flush()
        P.end_scope()

    def phase_B():
        P.scope()
        qd = [P.sbT([128, S], BF16, "qd") for _ in range(8)]
        ki = [P.sbT([128, S], BF16, "ki") for _ in range(8)]
        ke_h = P.sb([128, NT, 1024], BF16, "ke")
        ke = [P.T(ke_h[:, t, :]) for t in range(NT)]
        dec = [P.sbT([128, 16], F32, "dec") for _ in range(8)]
        P.scope()
        glr = P.sbT([32, S], F32, "glr")
        wfb = P.sbT([32, 1024], F32, "wfb")
        rmask = P.sbT([128, S], BF16, "rmask")
        memset(P, "pool", glr, glr[:, :], 1.0)
        dma(P, "sp", glr, glr[0:16, :], glrs_t, glrs_t[:, :])
        dma(P, "sp", wfb, wfb[0:16, :], WT, gla_w_fgate[:, :])
        dma(P, "sp", wfb, wfb[16:17, :], WT, gla_b_fgate.ap().rearrange("(o n) -> o n", o=1))
        memset(P, "pool", rmask, rmask[:, :], 1.0)
        memset(P, "pool", rmask, rmask[:, 0:S:128], 0.0)
        lt = Rot([P.sbT([128, S], F32, "lt") for _ in range(1)])
        cumr = Rot([P.sbT([128, S], F32, "cum") for _ in range(1)])
        e1r = Rot([P.sbT([128, S], F32, "e1") for _ in range(2)])
        e2r = Rot([P.sbT([128, S], F32, "e2") for _ in range(2)])
        qtr = Rot([P.sbT([128, S], F32, "qt") for _ in range(2)])
        ktr = Rot([P.sbT([128, S], F32, "kt") for _ in range(2)])
        keTr = Rot([P.sbT([128, S], BF16, "keT") for _ in range(2)])
        psr = Rot([0, 1, 2, 3])
        psr2 = Rot([4, 5, 6, 7])
        ee = {}

        def b1_gate(dt):
            l = lt.next()
            for c in range(4):
                pt = ps[psr.next()]
                mm(P, pt, pt[:, :], wfb, wfb[0:17, dt * 128:(dt + 1) * 128], glr, glr[0:17, c * 512:(c + 1) * 512], True, True)
                act(P, l, l[:, c * 512:(c + 1) * 512], pt, pt[:, :], AF.Exp, scale=-1.0)
            act(P, l, l[:, :], l, l[:, :], AF.Ln, bias=1.0)
            cum = cumr.next()
            P.add("dve", lambda e, cum=cum, l=l: e.tensor_tensor_scan(out=cum[:, :], data0=rmask[:, :], data1=l[:, :], initial=0.0,
                                                                      op0=ALU.mult, op1=ALU.add), reads=[rmask, l], writes=[cum])
            e1, e2 = e1r.next(), e2r.next()
            act(P, e1, e1[:, :], cum, cum[:, :], AF.Exp, scale=-1.0 / 16)
            act(P, e2, e2[:, :], cum, cum[:, :], AF.Exp, scale=1.0 / 16)
            cp(P, "pool", dec[dt], dec[dt][:, :], e1, e1[:, 127:S:128])
            qt, kt_ = qtr.next(), ktr.next()
            dma(P, "sp", qt, qt[:, :], qTs_t[dt], qTs_t[dt][:, :])
            dma(P, "sp", kt_, kt_[:, :], kTs_t[dt], kTs_t[dt][:, :])
            ee[dt] = (e1, e2, qt, kt_)

        def b1_apply(dt):
            e1, e2, qt, kt_ = ee.pop(dt)
            stt(P, qd[dt], qd[dt][:, :], qt, qt[:, :], 1.0 / 16, e1, e1[:, :], ALU.mult, ALU.mult)
            tt(P, "dve", kt_, kt_[:, :], kt_, kt_[:, :], e2, e2[:, :], ALU.mult)
            cp(P, "pool", ki[dt], ki[dt][:, :], kt_, kt_[:, :])
            keT = keTr.next()
            for t in range(NT):
                ts(P, "dve", keT, keT[:, t * 128:(t + 1) * 128], kt_, kt_[:, t * 128:(t + 1) * 128],
                   dec[dt][:, t:t + 1], None, ALU.mult, extra_reads=[dec[dt]])
            for half in range(2):
                pi = psr2.next()
                for k8 in range(8):
                    t = half * 8 + k8
                    tr(P, ps[pi], psb[pi][:, k8 * 128:(k8 + 1) * 128], keT, keT[:, t * 128:(t + 1) * 128], ident, ident[:, :])
                P.add("act", lambda e, pi=pi, half=half, dt=dt: e.activation(
                    out=ke_h[:, half * 8:half * 8 + 8, dt * 128:(dt + 1) * 128],
                    in_=psb[pi][:, 0:1024].rearrange("p (a b) -> p a b", b=128), func=AF.Copy),
                    reads=[ps[pi]], writes=ke[half * 8:half * 8 + 8])

        b1_gate(0)
        for dt in range(8):
            if dt + 1 < 8:
                b1_gate(dt + 1)
            b1_apply(dt)
        P.flush()
        P.end_scope()
        P.scope()
        gnb = P.sbT([128, 512], F32, "gnb")
        dma(P, "sp", gnb, gnb[:, :], WT, bass.AP(gla_norm_g, 0, [[0, 128], [1, 512]]))
        vr = Rot([P.sbT([128, D], BF16, "v") for _ in range(3)])
        srr = Rot([P.sbT([128, D], BF16, "sr") for _ in range(3)])
        state = [[P.sbT([128, 512], F32, "st") for _ in range(2)] for _ in range(4)]
        stb = [[Rot([P.sbT([128, 512], BF16, "stb") for _ in range(2)]) for _ in range(2)] for _ in range(4)]
        stb_cur = [[None, None] for _ in range(4)]
        attm_r = Rot([P.sbT([128, 128], BF16, "attm") for _ in range(4)])
        tmp_r = Rot([P.sbT([128, 512], F32, "tmp") for _ in range(3)])
        og_r = Rot([P.sbT([128, D], BF16, "og") for _ in range(2)])
        ogT_r = Rot([P.sbT([128, KT, 128], BF16, "ogT") for _ in range(2)])
        junk = P.sbT([128, 512], BF16, "junk")
        stats = stat_tiles(64)
        ps_att = Rot([0])
        ps_o = Rot([1, 2, 7])
        ps_su = Rot([3, 4])
        ps_tr = Rot([5, 6])
        ogT_view = ogT_s.ap().rearrange("(et p) s -> p et s", p=128)
        items = [(t, h) for t in range(NT) for h in range(4)]
        vt, srt, ogt, att_of = {}, {}, {}, {}

        def load(t):
            v, sr = vr.next(), srr.next()
            dma(P, "sp", v, v[:, :], vs_t[t], vs_t[t][:, :])
            dma(P, "sp", sr, sr[:, :], srs_t[t], srs_t[t][:, :])
            vt[t], srt[t], ogt[t] = v, sr, og_r.next()

        def att_stage(i):
            t, h = items[i]
            tc = slice(t * 128, (t + 1) * 128)
            pa = ps[ps_att.next()]
            for dl in range(2):
                dt = 2 * h + dl
                mm(P, pa, pa[:, 0:128], ki[dt], ki[dt][:, tc], qd[dt], qd[dt][:, tc], dl == 0, dl == 1)
            attm = attm_r.next()
            tt(P, "dve", attm, attm[:, :], pa, pa[:, 0:128], maskT, maskT[:, :], ALU.mult)
            att_of[i] = attm

        def main_stage(i):
            t, h = items[i]
            tc = slice(t * 128, (t + 1) * 128)
            hc = slice(h * 512, (h + 1) * 512)
            v = vt[t]
            attm = att_of.pop(i)
            po = ps[ps_o.next()]
            mm(P, po, po[:, :], attm, attm[:, :], v, v[:, hc], True, t == 0)
            if t > 0:
                for dl in range(2):
                    dt = 2 * h + dl
                    sb_ = stb_cur[h][dl]
                    mm(P, po, po[:, :], qd[dt], qd[dt][:, tc], sb_, sb_[:, :], False, dl == 1)
            if t < NT - 1:
                for dl in range(2):
                    dt = 2 * h + dl
                    pu = ps[ps_su.next()]
                    mm(P, pu, pu[:, :], ke[t], ke_h[:, t, dt * 128:(dt + 1) * 128], v, v[:, hc], True, True)
                    st_ = state[h][dl]
                    if t == 0:
                        cp(P, "dve", st_, st_[:, :], pu, pu[:, :])
                    else:
                        stt(P, st_, st_[:, :], st_, st_[:, :], dec[dt][:, t:t + 1], pu, pu[:, :], ALU.mult, ALU.add,
                            extra_reads=[dec[dt]])
                    nb = stb[h][dl].next()
                    cp(P, "act", nb, nb[:, :], st_, st_[:, :])
                    stb_cur[h][dl] = nb
            return po

        def epi_a(i, po):
            st = stats[i]
            act(P, junk, junk[:, :], po, po[:, :], AF.Square, accum=(st, st[:, 0:1]))
            rstd_ops(P, st, st[:, 0:1], st[:, 1:2], st[:, 2:3], 512, nh, nh[:, 0:1])

        def epi_stage(i, po):
            t, h = items[i]
            hc = slice(h * 512, (h + 1) * 512)
            st = stats[i]
            og, sr = ogt[t], srt[t]
            tmp = tmp_r.next()
            stt(P, tmp, tmp[:, :], po, po[:, :], st[:, 2:3], gnb, gnb[:, :], ALU.mult, ALU.mult, extra_reads=[st])
            tt(P, "pool", og, og[:, hc], tmp, tmp[:, :], sr, sr[:, hc], ALU.mult)
            if h == 3:
                tc = slice(t * 128, (t + 1) * 128)
                ogT = ogT_r.next()
                for half in range(2):
                    pi = ps_tr.next()
                    for k8 in range(8):
                        et = half * 8 + k8
                        tr(P, ps[pi], psb[pi][:, k8 * 128:(k8 + 1) * 128], og, og[:, et * 128:(et + 1) * 128], ident, ident[:, :])
                    cp(P, "act" if half == 0 else "dve", ogT, ogT[:, half * 8:half * 8 + 8, :], ps[pi],
                       psb[pi][:, 0:1024].rearrange("p (a b) -> p a b", b=128))
                dma(P, "pool", ogTs_t, ogT_view[:, :, tc], ogT, ogT[:, :, :])

        load(0)
        att_stage(0)
        prev = None
        for i in range(len(items)):
            t, h = items[i]
            if h == 0 and t + 1 < NT:
                load(t + 1)
            if i + 1 < len(items):
                att_stage(i + 1)
            if prev is not None:
                epi_a(*prev)
            po = main_stage(i)
            if prev is not None:
                epi_stage(*prev)
            prev = (i, po)
        epi_a(*prev)
        epi_stage(*prev)
        P.flush()
        P.end_scope()
        P.end_scope()

    def phase_linpost(inT_dram, inT_t, W2d, g1d, xold_t, xnew_t):
        P.scope()
        gb = load_gb(g1d)
        inT = P.sbT([128, KT, S], BF16, "inT")
        src = inT_dram.ap().rearrange("(kt p) s -> p kt s", p=128)
        inTq = [P.T(inT[:, :, q * 512:(q + 1) * 512]) for q in range(4)]
        for q in range(4):
            dma(P, "sp", inTq[q], inTq[q][:, :, :], inT_t, src[:, :, q * 512:(q + 1) * 512])
        slabs = [P.sbT([128, KT, 512], BF16, "w") for _ in range(4)]
        for cc in range(4):
            wslab(W2d, cc * 512, 512, slabs[cc], KT)
        xr = Rot([P.sbT([128, D], F32, "xo") for _ in range(2)])
        tmpr = Rot([P.sbT([128, D], F32, "tmp") for _ in range(2)])
        junk = P.sbT([128, 512], BF16, "junk")
        stats = stat_tiles(NT, 8)
        for t in range(NT):
            stat = stats[t]
            base = (t % 2) * 4
            xo = xr.next()
            dma(P, "sp", xo, xo[:, :], xold_t[t], xold_t[t][:, :])
            iq = inTq[t // 4]
            for cc in range(4):
                pt = ps[base + cc]
                for kt in range(KT):
                    mm(P, pt, pt[:, :], iq, inT[:, kt, t * 128:(t + 1) * 128], slabs[cc], slabs[cc][:, kt, :], kt == 0, kt == KT - 1)
                act(P, junk, junk[:, :], pt, pt[:, :], AF.Square, accum=(stat, stat[:, cc:cc + 1]))
            o = 0
            tt(P, "pool", stat, stat[:, o + 4:o + 5], stat, stat[:, o:o + 1], stat, stat[:, o + 1:o + 2], ALU.add)
            tt(P, "pool", stat, stat[:, o + 5:o + 6], stat, stat[:, o + 2:o + 3], stat, stat[:, o + 3:o + 4], ALU.add)
            tt(P, "pool", stat, stat[:, o + 4:o + 5], stat, stat[:, o + 4:o + 5], stat, stat[:, o + 5:o + 6], ALU.add)
            rstd_ops(P, stat, stat[:, o + 4:o + 5], stat[:, o + 6:o + 7], stat[:, o + 7:o + 8], D, nh, nh[:, 0:1])
            tmp = tmpr.next()
            for cc in range(4):
                pt = ps[base + cc]
                cs = slice(cc * 512, (cc + 1) * 512)
                stt(P, tmp, tmp[:, cs], pt, pt[:, :], stat[:, o + 7:o + 8], gb, gb[:, cs], ALU.mult, ALU.mult, extra_reads=[stat])
            tt(P, "pool", tmp, tmp[:, :], tmp, tmp[:, :], xo, xo[:, :], ALU.add)
            dma(P, "pool", xnew_t[t], xnew_t[t][:, :], tmp, tmp[:, :])
        P.flush()
        P.end_scope()

    def phase_ffn(layer, xold_t, xnew_t, do_d4=True):
        Wgu = ffn_w_gate_up[layer, :, :]
        Wd = ffn_w_down[layer, :, :]
        TB = 1024
        P.scope()
        for blk in range(S // TB):
            P.scope()
            hid = P.sbT([128, FT, TB], BF16, "hid")
            hidc = [P.T(hid[:, :, c * 512:(c + 1) * 512]) for c in range(2)]
            P.scope()
            gb = load_gb(pre_ffn_g[layer, :])
            hT = P.sb([128, KT, TB], BF16, "hT")
            hTc = chunk_tiles(hT, TB)
            xpool = Rot([P.sbT([128, D], F32, "xt") for _ in range(2)])
            xspool = Rot([P.sbT([128, D], BF16, "xs") for _ in range(3)])
            junk = P.sbT([128, D], BF16, "junk")
            prenorm_T(xold_t[blk * 8:(blk + 1) * 8], gb, hT, hTc, 8, xpool, xspool, junk, stat_tiles(8), Rot([0, 1]))
            SW = 256
            slabG = Rot([P.sbT([128, KT, SW], BF16, "sg") for _ in range(2)])
            slabU = Rot([P.sbT([128, KT, SW], BF16, "su") for _ in range(2)])
            sgr = Rot([P.sbT([128, 512], F32, "sil") for _ in range(3)])
            psg = Rot([2, 3, 4])
            psu = Rot([5, 6, 7])
            for s_ in range(DFF // SW):
                sg_, su_ = slabG.next(), slabU.next()
                wslab(Wgu, s_ * SW, SW, sg_, KT)
                wslab(Wgu, DFF + s_ * SW, SW, su_, KT)
                for f in range(SW // 128):
                    j = s_ * (SW // 128) + f
                    for c in range(2):
                        pg, pu = ps[psg.next()], ps[psu.next()]
                        for kt in range(KT):
                            mm(P, pg, pg[:, :], sg_, sg_[:, kt, f * 128:(f + 1) * 128], hTc[c], hT[:, kt, c * 512:(c + 1) * 512],
                               kt == 0, kt == KT - 1)
                        for kt in range(KT):
                            mm(P, pu, pu[:, :], su_, su_[:, kt, f * 128:(f + 1) * 128], hTc[c], hT[:, kt, c * 512:(c + 1) * 512],
                               kt == 0, kt == KT - 1)
                        sil = sgr.next()
                        act(P, sil, sil[:, :], pg, pg[:, :], AF.Silu)
                        tt(P, "dve", hidc[c], hid[:, j, c * 512:(c + 1) * 512], sil, sil[:, :], pu, pu[:, :], ALU.mult)
            P.flush()
            P.end_scope()
            P.scope()
            def mk_sd():
                h = P.sb([128, FT, 512], BF16, "sd")
                return (h, [P.T(h[:, p * 11:(p + 1) * 11, :]) for p in range(4)])
            slabD = Rot([mk_sd() for _ in range(2)])
            fevr = Rot([P.sbT([128, 512], F32, "fev") for _ in range(4)])
            junk = P.sbT([128, 512], BF16, "junk")
            psf = Rot([int(c) for c in os.environ.get("KBANKS", "01234567")] if debug else [0, 1, 2, 3, 4, 5, 6, 7])
            Wd_v = Wd.rearrange("(j p) n -> p j n", p=128)
            for cc in range(4):
                sd, sdp = slabD.next()
                for p in range(4):
                    dma(P, "pool", sdp[p], sd[:, p * 11:(p + 1) * 11, :], WT, Wd_v[:, p * 11:(p + 1) * 11, cc * 512:(cc + 1) * 512])
                for tl in range(8):
                    t = blk * 8 + tl
                    pf = ps[psf.next()]
                    for j in range(FT if "mm" not in SKIP else 0):
                        mm(P, pf, pf[:, :], hidc[tl // 4], hid[:, j, tl * 128:(tl + 1) * 128], sdp[j // 11], sd[:, j, :], j == 0, j == FT - 1)
                    st = fstats[t]
                    if "sq" not in SKIP:
                        act(P, junk, junk[:, :], pf, pf[:, :], AF.Square, accum=(st, st[:, cc:cc + 1]))
                    fev = fevr.next()
                    if "cp" not in SKIP:
                        cp(P, "dve", fev, fev[:, :], pf, pf[:, :])
                    if "st" not in SKIP:
                        dma(P, "sp", fscr_t[t], fscr_t[t][:, cc * 512:(cc + 1) * 512], fev, fev[:, :])
            P.flush()
            P.end_scope()
            P.end_scope()
        if not do_d4:
            P.end_scope()
            return
        P.scope()
        gb = load_gb(post_ffn_g[layer, :])
        xr = Rot([P.sbT([128, D], F32, "xo") for _ in range(3)])
        fr = Rot([P.sbT([128, D], F32, "fo") for _ in range(3)])
        for t in range(0 if "d4" not in SKIP else NT, NT):
            xo, fo = xr.next(), fr.next()
            dma(P, "sp", xo, xo[:, :], xold_t[t], xold_t[t][:, :])
            dma(P, "sp", fo, fo[:, :], fscr_t[t], fscr_t[t][:, :])
            stat = fstats[t]
            tt(P, "pool", stat, stat[:, 4:5], stat, stat[:, 0:1], stat, stat[:, 1:2], ALU.add)
            tt(P, "pool", stat, stat[:, 5:6], stat, stat[:, 2:3], stat, stat[:, 3:4], ALU.add)
            tt(P, "pool", stat, stat[:, 4:5], stat, stat[:, 4:5], stat, stat[:, 5:6], ALU.add)
            rstd_ops(P, stat, stat[:, 4:5], stat[:, 6:7], stat[:, 7:8], D, nh, nh[:, 0:1])
            stt(P, fo, fo[:, :], fo, fo[:, :], stat[:, 7:8], gb, gb[:, :], ALU.mult, ALU.mult, extra_reads=[stat])
            tt(P, "dve", fo, fo[:, :], fo, fo[:, :], xo, xo[:, :], ALU.add)
            dma(P, "pool", xnew_t[t], xnew_t[t][:, :], fo, fo[:, :])
        P.flush()
        P.end_scope()
        P.end_scope()

    def phase_E(fuse_ffn=True):
        for which in range(2):
            P.scope()
            gb = load_gb(kv_norm_g.ap() if which == 0 else pre_mix_g[1, :])
            hT = P.sb([128, KT, S], BF16, "hT")
            hTc = chunk_tiles(hT, S)
            xspool = Rot([P.sbT([128, D], BF16, "xs") for _ in range(3)])
            junk = P.sbT([128, D], BF16, "junk")
            if which == 0 and fuse_ffn:
                xpool = Rot([P.sbT([128, D], F32, "xt") for _ in range(3)])
                fuse = {"fpool": Rot([P.sbT([128, D], F32, "ft") for _ in range(2)]), "f": fscr_t, "stats": fstats,
                        "gpost": load_gb(post_ffn_g[0, :]), "xnew": x2_t}
                prenorm_T(x1_t, gb, hT, hTc, NT, xpool, xspool, junk, stat_tiles(NT), Rot([0, 1]), fuse=fuse)
            else:
                xpool = Rot([P.sbT([128, D], F32, "xt") for _ in range(2)])
                prenorm_T(x2_t, gb, hT, hTc, NT, xpool, xspool, junk, stat_tiles(NT), Rot([0, 1]))
            W = w_kv[:, :] if which == 0 else diff_w_q[:, :]
            dstT = kT2_t if which == 0 else qT2_t
            scale = 1.0 if which == 0 else 128.0 ** -0.5
            slabsB = Rot([P.sbT([128, KT, 256], BF16, "slabB") for _ in range(2)])
            evB = Rot([P.sbT([128, S], BF16, "evB") for _ in range(2)])
            psr = Rot([2, 3, 4, 5, 6, 7])
            for sidx in range(8):
                slab = slabsB.next()
                wslab(W, sidx * 256, 256, slab, KT)
                for n in range(2):
                    nt_idx = sidx * 2 + n
                    ev = evB.next()
                    for c in range(4):
                        pt = ps[psr.next()]
                        for kt in range(KT):
                            mm(P, pt, pt[:, :], slab, slab[:, kt, n * 128:(n + 1) * 128], hTc[c], hT[:, kt, c * 512:(c + 1) * 512],
                               kt == 0, kt == KT - 1)
                        if c % 2 == 0:
                            act(P, ev, ev[:, c * 512:(c + 1) * 512], pt, pt[:, :], AF.Copy, scale=scale)
                        else:
                            ts(P, "dve", ev, ev[:, c * 512:(c + 1) * 512], pt, pt[:, :], scale, None, ALU.mult)
                    dma(P, "sp", dstT[nt_idx], dstT[nt_idx][:, :], ev, ev[:, :])
            if which == 0:
                slabsA = Rot([P.sbT([128, KT, 512], BF16, "slabA") for _ in range(2)])
                evA = Rot([P.sbT([128, 512], BF16, "evA") for _ in range(4)])
                for ch in range(4):
                    slab = slabsA.next()
                    wslab(W, 2048 + ch * 512, 512, slab, KT)
                    for t in range(NT):
                        pt = ps[psr.next()]
                        for kt in range(KT):
                            mm(P, pt, pt[:, :], hTc[t // 4], hT[:, kt, t * 128:(t + 1) * 128], slab, slab[:, kt, :], kt == 0, kt == KT - 1)
                        ev = evA.next()
                        cp(P, "dve" if t % 2 == 0 else "act", ev, ev[:, :], pt, pt[:, :])
                        dma(P, "sp", v2_t[t], v2_t[t][:, ch * 512:(ch + 1) * 512], ev, ev[:, :])
            P.flush()
            P.end_scope()

    def phase_F():
        P.scope()
        Bh = P.sbT([128, 8, 256], F32, "Bh")
        dma(P, "sp", Bh, Bh[:, :, :], WT, bias_t[:, :, :])
        for h in range(8):
            P.add("pool", lambda e, h=h: e.affine_select(out=Bh[:, h, 0:128], in_=Bh[:, h, 0:128], pattern=[[1, 128]],
                                                         compare_op=ALU.is_ge, fill=MASKV, base=0, channel_multiplier=-1),
                  reads=[Bh], writes=[Bh])
        kTr = Rot([P.sbT([128, 2, S], BF16, "kTh") for _ in range(2)])
        qTr = Rot([P.sbT([128, 2, S], BF16, "qTh") for _ in range(2)])
        Vr = Rot([P.sbT([128, NT, 256], BF16, "Vh") for _ in range(2)])
        PTr = Rot([P.sbT([128, 512], BF16, "PT") for _ in range(6)])
        tbr = Rot([P.sbT([128, 256], F32, "tb") for _ in range(4)])
        rlr = [Rot([P.sbT([128, 512], F32, "rl") for _ in range(2)]) for _ in range(2)]
        ar = [Rot([P.sbT([128, 512], F32, "a") for _ in range(2)]) for _ in range(2)]
        t1r = Rot([P.sbT([128, 512], F32, "t1") for _ in range(4)])
        sqr = Rot([P.sbT([128, 512], BF16, "sq") for _ in range(4)])
        rsr = Rot([P.sbT([128, 512], F32, "rs") for _ in range(2)])
        oTr = Rot([P.sbT([128, 512], BF16, "oT") for _ in range(4)])
        ps_s = Rot([6, 7])
        psO = [[ps[0], ps[1]], [ps[2], ps[3]]]
        psL = [ps[4], ps[5]]
        kv_view = kT2.ap().rearrange("(n p) s -> p n s", p=128)
        q_view = qT2.ap().rearrange("(n p) s -> p n s", p=128)
        v_view = v2.ap().rearrange("(t p) e -> p t e", p=128)
        kT2_all = P.T(kT2[:, :])
        qT2_all = P.T(qT2[:, :])
        v2_all = P.T(v2[:, :])
        head_tiles = {}

        def load_head(h):
            kTh, qTh, Vh = kTr.next(), qTr.next(), Vr.next()
            dma(P, "sp", kTh, kTh[:, :, :], kT2_all, kv_view[:, 2 * h:2 * h + 2, :])
            dma(P, "sp", qTh, qTh[:, :, :], qT2_all, q_view[:, 2 * h:2 * h + 2, :])
            dma(P, "sp", Vh, Vh[:, :, :], v2_all, v_view[:, :, h * 256:(h + 1) * 256])
            head_tiles[h] = (kTh, qTh, Vh)

        steps = []
        for h in range(8):
            for qb in range(4):
                nj = 4 * qb + 4
                for j in range(nj):
                    for m in range(2):
                        steps.append((h, qb, j, m, nj))

        def qk_exp(s_):
            h, qb, j, m, nj = steps[s_]
            kTh, qTh, Vh = head_tiles[h]
            c0 = max(0, (j - 4 * qb) * 128)
            if j >= 4 * qb:
                nb = 2 if (j - 4 * qb) < 3 else 1
                b0 = 0
            elif j == 4 * qb - 1:
                nb, b0 = 1, 128
            else:
                nb, b0 = 0, 0
            pS = ps[ps_s.next()]
            mm(P, pS, pS[:, c0:512], kTh, kTh[:, m, j * 128:(j + 1) * 128], qTh, qTh[:, m, qb * 512 + c0:(qb + 1) * 512], True, True)
            PT = PTr.next()
            c1 = c0 + nb * 128
            if nb > 0:
                tb = tbr.next()
                tt(P, "dve", tb, tb[:, 0:nb * 128], pS, pS[:, c0:c1], Bh, Bh[:, h, b0:b0 + nb * 128], ALU.add)
            if c1 < 512:
                act(P, PT, PT[:, c1:512], pS, pS[:, c1:512], AF.Exp, bias=ctab_b[:, h:h + 1], extra_reads=[ctab_b])
            if nb > 0:
                act(P, PT, PT[:, c0:c1], tb, tb[:, 0:nb * 128], AF.Exp)
            return PT, c0

        def pv(s_, PT, c0):
            h, qb, j, m, nj = steps[s_]
            kTh, qTh, Vh = head_tiles[h]
            for e_ in range(2):
                po = psO[m][e_]
                mm(P, po, po[:, c0:512], Vh, Vh[:, j, e_ * 128:(e_ + 1) * 128], PT, PT[:, c0:512], j == 0, j == nj - 1)
            mm(P, psL[m], psL[m][:, c0:512], ones_b, ones_b[:, :], PT, PT[:, c0:512], j == 0, j == nj - 1)

        deferred = []

        def finalize(h, qb, s_now):
            rl = [rlr[0].next(), rlr[1].next()]
            for m in range(2):
                P.add("dve", lambda e, o=rl[m], i=psL[m]: e.reciprocal(out=o[:, :], in_=i[:, :]), reads=[psL[m]], writes=[rl[m]])
            a = [ar[0].next(), ar[1].next()]
            t1 = [t1r.next(), t1r.next()]
            for e_ in range(2):
                stt(P, t1[e_], t1[e_][:, :], psO[1][e_], psO[1][e_][:, :], neglam[:, 0:1], rl[1], rl[1][:, :], ALU.mult, ALU.mult,
                    extra_reads=[neglam])
                tt(P, "dve", a[e_], a[e_][:, :], psO[0][e_], psO[0][e_][:, :], rl[0], rl[0][:, :], ALU.mult)
            sq = [sqr.next(), sqr.next()]
            rs = rsr.next()

            def part2a():
                for e_ in range(2):
                    tt(P, "pool", a[e_], a[e_][:, :], a[e_], a[e_][:, :], t1[e_], t1[e_][:, :], ALU.add)
                    act(P, sq[e_], sq[e_][:, :], a[e_], a[e_][:, :], AF.Square)

            def part2b():
                pS = ps[ps_s.next()]
                for e_ in range(2):
                    mm(P, pS, pS[:, :], ones_b, ones_b[:, :], sq[e_], sq[e_][:, :], e_ == 0, e_ == 1)
                act(P, rs, rs[:, :], pS, pS[:, :], AF.Ln, scale=1.0 / 256, bias=EPS)
                act(P, rs, rs[:, :], rs, rs[:, :], AF.Exp, scale=-0.5)

            def part2c():
                for e_ in range(2):
                    oT = oTr.next()
                    stt(P, oT, oT[:, :], a[e_], a[e_][:, :], gsub[:, e_:e_ + 1], rs, rs[:, :], ALU.mult, ALU.mult, extra_reads=[gsub])
                    r0 = (2 * h + e_) * 128
                    dma(P, "pool", oTs_t, oT_s[r0:r0 + 128, qb * 512:(qb + 1) * 512], oT, oT[:, :])

            deferred.append((s_now + 1, part2a))
            deferred.append((s_now + 2, part2b))
            deferred.append((s_now + 4, part2c))

        pend = {}

        def do_pv(sp_, tick):
            h, qb, j, m, nj = steps[sp_]
            pv(sp_, *pend.pop(sp_))
            if j == nj - 1 and m == 1:
                finalize(h, qb, tick)

        load_head(0)
        pend[0] = qk_exp(0)
        n_steps = len(steps)
        for s_ in range(n_steps):
            h, qb, j, m, nj = steps[s_]
            if s_ + 1 < n_steps:
                pend[s_ + 1] = qk_exp(s_ + 1)
            if s_ >= 1:
                do_pv(s_ - 1, s_)
            if qb == 0 and j == 0 and m == 1 and h + 1 < 8:
                load_head(h + 1)
            while deferred and deferred[0][0] <= s_:
                deferred.pop(0)[1]()
        do_pv(n_steps - 1, n_steps)
        while deferred:
            deferred.pop(0)[1]()
        P.flush()
        P.end_scope()

    import os
    stop = int(os.environ.get("KSTOP", "99")) if debug else 99
    steps = [
        phase_A,
        phase_B,
        lambda: phase_linpost(ogT_s, ogTs_t, gla_w_out[:, :], post_mix_g[0, :], xin_t, x1_t),
        lambda: phase_ffn(0, x1_t, x2_t, do_d4=False),
        phase_E,
        phase_F,
        lambda: phase_linpost(oT_s, oTs_t, diff_w_out[:, :], post_mix_g[1, :], x2_t, x3_t),
        lambda: phase_ffn(1, x3_t, y_t),
    ]
    for i, st_ in enumerate(steps):
        if i < stop:
            st_()
    print("instructions:", P.n_inst)
    return nc


def _t5_bucket_np(dist):
    n = np.maximum(dist, 0)
    nf = np.maximum(n, 1).astype(np.float32)
    large = 16 + (np.log(nf / 16) / np.float32(math.log(128 / 16)) * 16).astype(np.int32)
    large = np.minimum(large, 31)
    return np.where(n < 16, n, large)


def prep_inputs(inputs):
    f = lambda a: np.ascontiguousarray(np.asarray(a, dtype=np.float32))
    table = f(inputs["rel_bias_table"])
    k = np.arange(128)[:, None]
    c = np.arange(256)[None, :]
    idx = _t5_bucket_np(c - k)
    bias_t = np.ascontiguousarray(np.transpose(table[idx], (0, 2, 1)))
    shared = {
        "bias_t": bias_t,
        "ctab": f(table[31, :]),
        "kv_norm_g": f(inputs["kv_norm_g"]),
        "w_kv": f(inputs["w_kv"]),
        "gla_w_in": f(inputs["gla_w_in"][0]),
        "gla_w_fgate": f(inputs["gla_w_fgate"][0]),
        "gla_b_fgate": f(inputs["gla_b_fgate"][0]),
        "gla_norm_g": f(inputs["gla_norm_g"][0]),
        "gla_w_out": f(inputs["gla_w_out"][0]),
        "diff_w_q": f(inputs["diff_w_q"][0]),
        "lam4": f(np.stack([inputs["diff_lam_q1"][0], inputs["diff_lam_k1"][0], inputs["diff_lam_q2"][0], inputs["diff_lam_k2"][0]])),
        "diff_subln_g": f(inputs["diff_subln_g"][0]),
        "diff_w_out": f(inputs["diff_w_out"][0]),
        "pre_mix_g": f(inputs["pre_mix_g"]),
        "post_mix_g": f(inputs["post_mix_g"]),
        "pre_ffn_g": f(inputs["pre_ffn_g"]),
        "post_ffn_g": f(inputs["post_ffn_g"]),
        "ffn_w_gate_up": f(inputs["ffn_w_gate_up"]),
        "ffn_w_down": f(inputs["ffn_w_down"]),
    }
    x = f(inputs["x"])
    return [dict(shared, x=x[b]) for b in range(x.shape[0])]


_NC_CACHE = {}


def kernel(**inputs):
    in_maps = prep_inputs(inputs)
    if "nc" not in _NC_CACHE:
        _NC_CACHE["nc"] = build_nc()
    nc = _NC_CACHE["nc"]
    res = run_bass_kernel_spmd(nc, in_maps, core_ids=list(range(8)))
    return np.stack([np.asarray(r["y"], dtype=np.float32) for r in res.results], axis=0)
```

```python
import contextlib
import math
import numpy as np
import concourse.bass as bass
import concourse.mybir as mybir
from concourse.bass_utils import run_bass_kernel_spmd

F32 = mybir.dt.float32
BF16 = mybir.dt.bfloat16
AF = mybir.ActivationFunctionType
ALU = mybir.AluOpType

S = 2048
D = 2048
NT = S // 128
KT = D // 128
DFF = 5632
FT = DFF // 128
GLA_IN = 6160
EPS = 1e-6
LAMBDA_INIT = 0.8 - 0.6 * math.exp(-0.3 * 1)
MASKV = -30000.0


class Tl:
    __slots__ = ("ap", "w", "r", "dsem", "psum")

    def __init__(self, ap):
        self.ap = ap
        self.w = None
        self.r = []
        self.dsem = None
        self.psum = "PSum" in type(ap.tensor).__name__

    def __getitem__(self, k):
        return self.ap[k]


class Op:
    __slots__ = ("q", "fn", "deps", "seq", "need", "dma", "dsem", "dval")

    def __init__(self, q, fn, dma):
        self.q = q
        self.fn = fn
        self.deps = {}
        self.seq = None
        self.need = False
        self.dma = dma
        self.dsem = None
        self.dval = None


QUEUES = ("pe", "act", "dve", "pool", "sp")
import os as _os
STRICT = _os.environ.get("KSTRICT", "0") == "1"


class Prog:
    def __init__(self, nc):
        self.nc = nc
        self.ops = {q: [] for q in QUEUES}
        self.qsem = {q: nc.alloc_semaphore("q_" + q) for q in QUEUES}
        self.qcount = {q: 0 for q in QUEUES}
        self.seen = {q: {} for q in QUEUES}
        self.free_dsems = {"hw": [nc.alloc_semaphore("dh%d" % i) for i in range(45)],
                           "sw": [nc.alloc_semaphore("ds%d" % i) for i in range(45)]}
        self.semval = {}
        self.tiles = []
        self.phase_dsems = []
        self.stacks = []
        self.n_inst = 0
        self.uid = 0

    def scope(self):
        st = contextlib.ExitStack()
        self.stacks.append(st)
        return st

    def end_scope(self):
        self.stacks.pop().close()

    def sb(self, shape, dtype, name=None):
        self.uid += 1
        h = self.stacks[-1].enter_context(self.nc.sbuf_tensor("%s_%d" % (name or "t", self.uid), list(shape), dtype))
        return h

    def T(self, ap):
        t = Tl(ap)
        self.tiles.append(t)
        return t

    def sbT(self, shape, dtype, name=None):
        h = self.sb(shape, dtype, name)
        return self.T(h[tuple(slice(None) for _ in shape)])

    def add(self, q, fn, reads=(), writes=(), dma=False):
        op = Op(q, fn, dma)
        for t in reads:
            if t.w is not None:
                op.deps[t.w] = True
            if t.psum:
                for r in t.r:
                    if r.q != q:
                        op.deps.setdefault(r, False)
        for t in writes:
            if t.w is not None:
                op.deps.setdefault(t.w, False)
            for r in t.r:
                op.deps.setdefault(r, False)
        if dma:
            st = None
            for t in list(writes) + list(reads):
                if t.dsem is not None or _is_sbuf(t):
                    st = t
                    break
            assert st is not None
            kind = "sw" if q == "pool" else "hw"
            if st.dsem is None:
                st.dsem = {}
            if kind not in st.dsem:
                sem_ = self.free_dsems[kind].pop()
                st.dsem[kind] = sem_
                self.phase_dsems.append((st, kind, sem_))
                self.semval.setdefault(sem_, 0)
            sem_ = st.dsem[kind]
            self.semval[sem_] += 16
            op.dsem = sem_
            op.dval = self.semval[sem_]
        for t in reads:
            t.r.append(op)
        for t in writes:
            t.w = op
            t.r = []
        self.ops[q].append(op)
        return op

    def flush(self):
        nc = self.nc
        drain = [(sem, self.semval[sem]) for (_, _, sem) in self.phase_dsems]
        for q in QUEUES:
            for op in self.ops[q]:
                for d, raw in op.deps.items():
                    if d.dma:
                        continue
                    if d.q == op.q and not op.dma and not raw and not (STRICT and op.q != "pe"):
                        continue
                    d.need = True
        for q in QUEUES:
            for op in self.ops[q]:
                if op.need and not op.dma:
                    self.qcount[q] += 1
                    op.seq = self.qcount[q]
        engs = {"pe": "tensor", "act": "scalar", "dve": "vector", "pool": "gpsimd", "sp": "sync"}

        def emit(q, eng):
            seen = self.seen[q]
            for op in self.ops[q]:
                w = {}
                for d, raw in op.deps.items():
                    if d.dma:
                        sem, val = d.dsem, d.dval
                    else:
                        if d.q == op.q and not op.dma and not raw and not (STRICT and op.q != "pe"):
                            continue
                        sem, val = self.qsem[d.q], d.seq
                    if w.get(sem, 0) < val:
                        w[sem] = val
                for sem, val in w.items():
                    if seen.get(sem, 0) < val:
                        eng.wait_ge(sem, val)
                        seen[sem] = val
                ins = op.fn(eng)
                self.n_inst += 1
                if op.dma:
                    ins.then_inc(op.dsem, 16)
                elif op.need:
                    ins.then_inc(self.qsem[q], 1)
            if q == "sp":
                for sem, val in drain:
                    if seen.get(sem, 0) < val:
                        eng.wait_ge(sem, val)
                        seen[sem] = val

        with nc.Block() as blk:
            for q in QUEUES:
                if self.ops[q] or q == "sp":
                    getattr(blk, engs[q])(lambda eng, q=q: emit(q, eng))
        for q in QUEUES:
            self.ops[q] = []
        for t, kind, sem in self.phase_dsems:
            t.dsem = None
            self.free_dsems[kind].append(sem)
        self.phase_dsems = []
        for t in self.tiles:
            t.w = None
            t.r = []
        self.tiles = [t for t in self.tiles if not _is_sbuf(t) or True]


def _is_sbuf(t):
    return "SBTensor" in type(t.ap.tensor).__name__


def dma(P, q, out_t, out_ap, in_t, in_ap):
    return P.add(q, lambda e: e.dma_start(out=out_ap, in_=in_ap), reads=[in_t], writes=[out_t], dma=True)


def mm(P, ps_t, out_ap, lhsT_t, lhsT_ap, rhs_t, rhs_ap, start, stop):
    return P.add("pe", lambda e: e.matmul(out_ap, lhsT=lhsT_ap, rhs=rhs_ap, start=start, stop=stop),
                 reads=[lhsT_t, rhs_t], writes=[ps_t])


def tr(P, ps_t, out_ap, in_t, in_ap, id_t, id_ap):
    return P.add("pe", lambda e: e.transpose(out=out_ap, in_=in_ap, identity=id_ap), reads=[in_t, id_t], writes=[ps_t])


def act(P, out_t, out_ap, in_t, in_ap, func, bias=None, scale=None, accum=None, extra_reads=(), q="act"):
    kw = {}
    if bias is not None:
        kw["bias"] = bias
    if scale is not None:
        kw["scale"] = scale
    wr = [out_t]
    if accum is not None:
        kw["accum_out"] = accum[1]
        wr.append(accum[0])
    return P.add("act", lambda e: e.activation(out=out_ap, in_=in_ap, func=func, **kw),
                 reads=[in_t] + list(extra_reads), writes=wr)


def tt(P, q, out_t, out_ap, a_t, a_ap, b_t, b_ap, op):
    return P.add(q, lambda e: e.tensor_tensor(out=out_ap, in0=a_ap, in1=b_ap, op=op), reads=[a_t, b_t], writes=[out_t])


def ts(P, q, out_t, out_ap, a_t, a_ap, s1, s2, op0, op1=None, extra_reads=()):
    if op1 is None:
        return P.add(q, lambda e: e.tensor_scalar(out=out_ap, in0=a_ap, scalar1=s1, scalar2=None, op0=op0),
                     reads=[a_t] + list(extra_reads), writes=[out_t])
    return P.add(q, lambda e: e.tensor_scalar(out=out_ap, in0=a_ap, scalar1=s1, scalar2=s2, op0=op0, op1=op1),
                 reads=[a_t] + list(extra_reads), writes=[out_t])


def stt(P, out_t, out_ap, a_t, a_ap, scalar, b_t, b_ap, op0, op1, extra_reads=()):
    return P.add("dve", lambda e: e.scalar_tensor_tensor(out=out_ap, in0=a_ap, scalar=scalar, in1=b_ap, op0=op0, op1=op1),
                 reads=[a_t, b_t] + list(extra_reads), writes=[out_t])


def cp(P, q, out_t, out_ap, in_t, in_ap):
    if q == "act":
        return P.add("act", lambda e: e.activation(out=out_ap, in_=in_ap, func=AF.Copy), reads=[in_t], writes=[out_t])
    return P.add(q, lambda e: e.tensor_copy(out=out_ap, in_=in_ap), reads=[in_t], writes=[out_t])


def memset(P, q, t, ap, val):
    return P.add(q, lambda e: e.memset(ap, val), writes=[t])


def rstd_ops(P, st_t, ss_ap, tmp_ap, out_ap, n, nh_t, nh_ap):
    ts(P, "pool", st_t, tmp_ap, st_t, ss_ap, 1.0 / n, EPS, ALU.mult, ALU.add)
    P.add("pool", lambda e: e.tensor_tensor(out=out_ap, in0=tmp_ap, in1=nh_ap, op=ALU.pow), reads=[st_t, nh_t], writes=[st_t])


class Rot:
    def __init__(self, items):
        self.items = items
        self.i = 0

    def next(self):
        t = self.items[self.i % len(self.items)]
        self.i += 1
        return t


def build_nc(debug=False):
    import os
    SKIP = set(os.environ.get("KSKIP", "").split(",")) if debug else set()
    nc = bass.Bass("TRN2", target_bir_lowering=False)
    P = Prog(nc)

    def din(name, shape, dt=F32):
        return nc.dram_tensor(name, list(shape), dt, kind="ExternalInput")

    def dscr(name, shape, dt=F32, out=False):
        return nc.dram_tensor(name, list(shape), dt, kind=("ExternalOutput" if out else "Internal"))

    x_in = din("x", [S, D])
    bias_t = din("bias_t", [128, 8, 256])
    ctab = din("ctab", [8])
    kv_norm_g = din("kv_norm_g", [D])
    w_kv = din("w_kv", [D, 2 * D])
    gla_w_in = din("gla_w_in", [D, GLA_IN])
    gla_w_fgate = din("gla_w_fgate", [16, 1024])
    gla_b_fgate = din("gla_b_fgate", [1024])
    gla_norm_g = din("gla_norm_g", [512])
    gla_w_out = din("gla_w_out", [D, D])
    diff_w_q = din("diff_w_q", [D, D])
    lam4 = din("lam4", [4, 128])
    diff_subln_g = din("diff_subln_g", [256])
    diff_w_out = din("diff_w_out", [D, D])
    pre_mix_g = din("pre_mix_g", [2, D])
    post_mix_g = din("post_mix_g", [2, D])
    pre_ffn_g = din("pre_ffn_g", [2, D])
    post_ffn_g = din("post_ffn_g", [2, D])
    ffn_w_gate_up = din("ffn_w_gate_up", [2, D, 2 * DFF])
    ffn_w_down = din("ffn_w_down", [2, DFF, D])

    x1 = dscr("x1", [S, D], out=debug)
    x2 = dscr("x2", [S, D], out=debug)
    x3 = dscr("x3", [S, D], out=debug)
    y = dscr("y", [S, D], out=True)
    fscr = dscr("fscr", [S, D])
    qT_s = fscr[0:1024, :]
    kT_s = fscr[1024:2048, :]
    v_s = dscr("v_s", [S, D], BF16)
    sr_s = dscr("sr_s", [S, D], BF16)
    glr_s = dscr("glr_s", [16, S])
    ogT_s = dscr("ogT_s", [D, S], BF16)
    kT2 = dscr("kT2", [D, S], BF16)
    qT2 = dscr("qT2", [D, S], BF16)
    v2 = dscr("v2", [S, D], BF16)
    oT_s = dscr("oT_s", [D, S], BF16)

    def rowtiles(h):
        return [P.T(h[i * 128:(i + 1) * 128, :]) for i in range(h.shape[0] // 128)]

    xin_t = rowtiles(x_in)
    x1_t, x2_t, x3_t, y_t = rowtiles(x1), rowtiles(x2), rowtiles(x3), rowtiles(y)
    WT = P.T(gla_w_in[:, :])
    qTs_t, kTs_t = rowtiles(qT_s), rowtiles(kT_s)
    vs_t, srs_t = rowtiles(v_s), rowtiles(sr_s)
    glrs_t = P.T(glr_s[:, :])
    ogTs_t = P.T(ogT_s[:, :])
    kT2_t, qT2_t = rowtiles(kT2), rowtiles(qT2)
    v2_t = rowtiles(v2)
    oTs_t = P.T(oT_s[:, :])
    fscr_t = rowtiles(fscr)

    g0 = P.scope()
    ps = [P.T(nc.alloc_psum_tensor("ps%d" % i, [128, 512], F32)[:, :]) for i in range(8)]
    psb = [t.ap.tensor.bitcast(BF16) for t in ps]
    ident = P.sbT([128, 128], BF16, "ident")
    identf = P.sbT([128, 128], F32, "identf")
    ones_b = P.sbT([128, 128], BF16, "ones_b")
    ones_f = P.sbT([128, 128], F32, "ones_f")
    maskT = P.sbT([128, 128], F32, "maskT")
    nh = P.sbT([128, 512], F32, "nh")
    neglam = P.sbT([128, 4], F32, "neglam")
    gsub = P.sbT([128, 2], F32, "gsub")
    lamt = P.sbT([128, 4], F32, "lamt")
    ctab_b = P.sbT([128, 8], F32, "ctab_b")
    _fs_h = P.sb([128, NT * 8], F32, "fstats")
    fstats = [P.T(_fs_h[:, i * 8:(i + 1) * 8]) for i in range(NT)]

    memset(P, "pool", identf, identf[:, :], 1.0)
    P.add("pool", lambda e: e.affine_select(out=identf[:, :], in_=identf[:, :], pattern=[[1, 128]], compare_op=ALU.is_equal,
                                             fill=0.0, base=0, channel_multiplier=-1), reads=[identf], writes=[identf])
    cp(P, "pool", ident, ident[:, :], identf, identf[:, :])
    memset(P, "pool", ones_b, ones_b[:, :], 1.0)
    memset(P, "pool", ones_f, ones_f[:, :], 1.0)
    memset(P, "pool", nh, nh[:, :], -0.5)
    memset(P, "pool", maskT, maskT[:, :], 1.0)
    P.add("pool", lambda e: e.affine_select(out=maskT[:, :], in_=maskT[:, :], pattern=[[1, 128]], compare_op=ALU.is_ge,
                                             fill=0.0, base=0, channel_multiplier=-1), reads=[maskT], writes=[maskT])
    for i in range(4):
        dma(P, "sp", lamt, lamt[:, i:i + 1], WT, lam4[i, :].rearrange("(p o) -> p o", o=1))
    for i in range(2):
        dma(P, "sp", gsub, gsub[:, i:i + 1], WT, diff_subln_g[i * 128:(i + 1) * 128].rearrange("(p o) -> p o", o=1))
    dma(P, "sp", ctab_b, ctab_b[:, :], WT, bass.AP(ctab, 0, [[0, 128], [1, 8]]))
    tt(P, "dve", neglam, neglam[:, 0:1], lamt, lamt[:, 0:1], lamt, lamt[:, 1:2], ALU.mult)
    tt(P, "dve", neglam, neglam[:, 1:2], lamt, lamt[:, 2:3], lamt, lamt[:, 3:4], ALU.mult)
    mm(P, ps[0], ps[0][:, 0:2], ones_f, ones_f[:, :], neglam, neglam[:, 0:2], True, True)
    act(P, neglam, neglam[:, 2:4], ps[0], ps[0][:, 0:2], AF.Exp)
    tt(P, "dve", neglam, neglam[:, 0:1], neglam, neglam[:, 3:4], neglam, neglam[:, 2:3], ALU.subtract)
    ts(P, "dve", neglam, neglam[:, 0:1], neglam, neglam[:, 0:1], -LAMBDA_INIT, None, ALU.add)
    ts(P, "dve", gsub, gsub[:, 0:2], gsub, gsub[:, 0:2], 1.0 - LAMBDA_INIT, None, ALU.mult)
    P.flush()

    def load_gb(g_ap_1d):
        gb = P.sbT([128, D], F32, "gb")
        n = g_ap_1d.shape[0]
        dma(P, "sp", gb, gb[:, 0:n], WT, bass.AP(g_ap_1d.tensor, g_ap_1d.offset, [[0, 128], [1, n]]))
        return gb

    def stat_tiles(n, w=4):
        h = P.sb([128, n * w], F32, "stat")
        return [P.T(h[:, i * w:(i + 1) * w]) for i in range(n)]

    def chunk_tiles(h, ncols):
        return [P.T(h[:, :, c * 512:(c + 1) * 512]) for c in range(ncols // 512)]

    def prenorm_T(xtiles, gb, hT, hTc, nt, xpool, xspool, junk, stats, psr, fuse=None):
        xs_of = {}

        def stage_a(li):
            xt = xpool.next()
            st = stats[li]
            dma(P, "sp", xt, xt[:, :], xtiles[li], xtiles[li][:, :])
            if fuse is not None:
                ft = fuse["fpool"].next()
                dma(P, "sp", ft, ft[:, :], fuse["f"][li], fuse["f"][li][:, :])
                fs = fuse["stats"][li]
                gp = fuse["gpost"]
                tt(P, "pool", fs, fs[:, 4:5], fs, fs[:, 0:1], fs, fs[:, 1:2], ALU.add)
                tt(P, "pool", fs, fs[:, 5:6], fs, fs[:, 2:3], fs, fs[:, 3:4], ALU.add)
                tt(P, "pool", fs, fs[:, 4:5], fs, fs[:, 4:5], fs, fs[:, 5:6], ALU.add)
                rstd_ops(P, fs, fs[:, 4:5], fs[:, 6:7], fs[:, 7:8], D, nh, nh[:, 0:1])
                stt(P, ft, ft[:, :], ft, ft[:, :], fs[:, 7:8], gp, gp[:, :], ALU.mult, ALU.mult, extra_reads=[fs])
                tt(P, "dve", xt, xt[:, :], xt, xt[:, :], ft, ft[:, :], ALU.add)
                dma(P, "pool", fuse["xnew"][li], fuse["xnew"][li][:, :], xt, xt[:, :])
            act(P, junk, junk[:, :], xt, xt[:, :], AF.Square, accum=(st, st[:, 0:1]))
            rstd_ops(P, st, st[:, 0:1], st[:, 1:2], st[:, 2:3], D, nh, nh[:, 0:1])
            xs = xspool.next()
            stt(P, xs, xs[:, :], xt, xt[:, :], st[:, 2:3], gb, gb[:, :], ALU.mult, ALU.mult, extra_reads=[st])
            xs_of[li] = xs

        def stage_b(li):
            xs = xs_of.pop(li)
            for half in range(2):
                pi = psr.next()
                pt, pb = ps[pi], psb[pi]
                for k in range(8):
                    kt = half * 8 + k
                    tr(P, pt, pb[:, k * 128:(k + 1) * 128], xs, xs[:, kt * 128:(kt + 1) * 128], ident, ident[:, :])
                q = "act" if half == 0 else "dve"
                cp(P, q, hTc[li // 4], hT[:, half * 8:half * 8 + 8, li * 128:(li + 1) * 128], pt,
                   pb[:, 0:1024].rearrange("p (a b) -> p a b", b=128))

        stage_a(0)
        for li in range(nt):
            if li + 1 < nt:
                stage_a(li + 1)
            stage_b(li)

    def wslab(W2d, c0, ncols, slab, kts):
        src = W2d.rearrange("(kt p) n -> p kt n", p=128)[:, :, c0:c0 + ncols]
        dma(P, "pool", slab, slab[:, 0:kts, 0:ncols], WT, src)

    def phase_A():
        P.scope()
        w_in = gla_w_in[:, :]
        gb = load_gb(pre_mix_g[0, :])
        hT = P.sb([128, KT, S], BF16, "hT")
        hTc = chunk_tiles(hT, S)
        xpool = Rot([P.sbT([128, D], F32, "xt") for _ in range(2)])
        xspool = Rot([P.sbT([128, D], BF16, "xs") for _ in range(3)])
        junk = P.sbT([128, D], BF16, "junk")
        prenorm_T(xin_t, gb, hT, hTc, NT, xpool, xspool, junk, stat_tiles(NT), Rot([0, 1]))
        psr = Rot([2, 3, 4, 5, 6, 7])
        slabsA = Rot([P.sbT([128, KT, 512], BF16, "slabA") for _ in range(2)])
        evA = Rot([P.sbT([128, 512], BF16, "evA") for _ in range(4)])
        for ch in range(8):
            slab = slabsA.next()
            wslab(w_in, 2048 + ch * 512, 512, slab, KT)
            is_v = ch < 4
            for t in range(NT):
                pt = ps[psr.next()]
                for kt in range(KT):
                    mm(P, pt, pt[:, :], hTc[t // 4], hT[:, kt, t * 128:(t + 1) * 128], slab, slab[:, kt, :], kt == 0, kt == KT - 1)
                ev = evA.next()
                if is_v:
                    cp(P, "dve", ev, ev[:, :], pt, pt[:, :])
                    dst = vs_t[t]
                    dma(P, "sp", dst, dst[:, ch * 512:(ch + 1) * 512], ev, ev[:, :])
                else:
                    act(P, ev, ev[:, :], pt, pt[:, :], AF.Silu)
                    dst = srs_t[t]
                    dma(P, "sp", dst, dst[:, (ch - 4) * 512:(ch - 3) * 512], ev, ev[:, :])
        slabsB = Rot([P.sbT([128, KT, 256], BF16, "slabB") for _ in range(2)])
        evB = Rot([P.sbT([128, S], F32, "evB") for _ in range(2)])
        for sidx in range(8):
            slab = slabsB.next()
            wslab(w_in, sidx * 256, 256, slab, KT)
            for n in range(2):
                nt_idx = sidx * 2 + n
                ev = evB.next()
                for c in range(4):
                    pt = ps[psr.next()]
                    for kt in range(KT):
                        mm(P, pt, pt[:, :], slab, slab[:, kt, n * 128:(n + 1) * 128], hTc[c], hT[:, kt, c * 512:(c + 1) * 512],
                           kt == 0, kt == KT - 1)
                    cp(P, "act" if c % 2 == 0 else "dve", ev, ev[:, c * 512:(c + 1) * 512], pt, pt[:, :])
                dst = qTs_t[nt_idx] if nt_idx < 8 else kTs_t[nt_idx - 8]
                dma(P, "sp", dst, dst[:, :], ev, ev[:, :])
        slab = slabsB.next()
        wslab(w_in, 6144, 16, slab, KT)
        ev = evB.next()
        for c in range(4):
            pt = ps[psr.next()]
            for kt in range(KT):
                mm(P, pt, pt[0:16, :], slab, slab[:, kt, 0:16], hTc[c], hT[:, kt, c * 512:(c + 1) * 512], kt == 0, kt == KT - 1)
            cp(P, "act", ev, ev[0:16, c * 512:(c + 1) * 512], pt, pt[0:16, :])
        dma(P, "sp", glrs_t, glrs_t[:, :], ev, ev[0:16, :])
        P.flush()
        P.end_scope()

    def phase_B():
        P.scope()
        qd = [P.sbT([128, S], BF16, "qd") for _ in range(8)]
        ki = [P.sbT([128, S], BF16, "ki") for _ in range(8)]
        ke_h = P.sb([128, NT, 1024], BF16, "ke")
        ke = [P.T(ke_h[:, t, :]) for t in range(NT)]
        dec = [P.sbT([128, 16], F32, "dec") for _ in range(8)]
        P.scope()
        glr = P.sbT([32, S], F32, "glr")
        wfb = P.sbT([32, 1024], F32, "wfb")
        rmask = P.sbT([128, S], BF16, "rmask")
        memset(P, "pool", glr, glr[:, :], 1.0)
        dma(P, "sp", glr, glr[0:16, :], glrs_t, glrs_t[:, :])
        dma(P, "sp", wfb, wfb[0:16, :], WT, gla_w_fgate[:, :])
        dma(P, "sp", wfb, wfb[16:17, :], WT, gla_b_fgate.ap().rearrange("(o n) -> o n", o=1))
        memset(P, "pool", rmask, rmask[:, :], 1.0)
        memset(P, "pool", rmask, rmask[:, 0:S:128], 0.0)
        lt = Rot([P.sbT([128, S], F32, "lt") for _ in range(1)])
        cumr = Rot([P.sbT([128, S], F32, "cum") for _ in range(1)])
        e1r = Rot([P.sbT([128, S], F32, "e1") for _ in range(2)])
        e2r = Rot([P.sbT([128, S], F32, "e2") for _ in range(2)])
        qtr = Rot([P.sbT([128, S], F32, "qt") for _ in range(2)])
        ktr = Rot([P.sbT([128, S], F32, "kt") for _ in range(2)])
        keTr = Rot([P.sbT([128, S], BF16, "keT") for _ in range(2)])
        psr = Rot([0, 1, 2, 3])
        psr2 = Rot([4, 5, 6, 7])
        ee = {}

        def b1_gate(dt):
            l = lt.next()
            for c in range(4):
                pt = ps[psr.next()]
                mm(P, pt, pt[:, :], wfb, wfb[0:17, dt * 128:(dt + 1) * 128], glr, glr[0:17, c * 512:(c + 1) * 512], True, True)
                act(P, l, l[:, c * 512:(c + 1) * 512], pt, pt[:, :], AF.Exp, scale=-1.0)
            act(P, l, l[:, :], l, l[:, :], AF.Ln, bias=1.0)
            cum = cumr.next()
            P.add("dve", lambda e, cum=cum, l=l: e.tensor_tensor_scan(out=cum[:, :], data0=rmask[:, :], data1=l[:, :], initial=0.0,
                                                                      op0=ALU.mult, op1=ALU.add), reads=[rmask, l], writes=[cum])
            e1, e2 = e1r.next(), e2r.next()
            act(P, e1, e1[:, :], cum, cum[:, :], AF.Exp, scale=-1.0 / 16)
            act(P, e2, e2[:, :], cum, cum[:, :], AF.Exp, scale=1.0 / 16)
            cp(P, "pool", dec[dt], dec[dt][:, :], e1, e1[:, 127:S:128])
            qt, kt_ = qtr.next(), ktr.next()
            dma(P, "sp", qt, qt[:, :], qTs_t[dt], qTs_t[dt][:, :])
            dma(P, "sp", kt_, kt_[:, :], kTs_t[dt], kTs_t[dt][:, :])
            ee[dt] = (e1, e2, qt, kt_)

        def b1_apply(dt):
            e1, e2, qt, kt_ = ee.pop(dt)
            stt(P, qd[dt], qd[dt][:, :], qt, qt[:, :], 1.0 / 16, e1, e1[:, :], ALU.mult, ALU.mult)
            tt(P, "dve", kt_, kt_[:, :], kt_, kt_[:, :], e2, e2[:, :], ALU.mult)
            cp(P, "pool", ki[dt], ki[dt][:, :], kt_, kt_[:, :])
            keT = keTr.next()
            for t in range(NT):
                ts(P, "dve", keT, keT[:, t * 128:(t + 1) * 128], kt_, kt_[:, t * 128:(t + 1) * 128],
                   dec[dt][:, t:t + 1], None, ALU.mult, extra_reads=[dec[dt]])
            for half in range(2):
                pi = psr2.next()
                for k8 in range(8):
                    t = half * 8 + k8
                    tr(P, ps[pi], psb[pi][:, k8 * 128:(k8 + 1) * 128], keT, keT[:, t * 128:(t + 1) * 128], ident, ident[:, :])
                P.add("act", lambda e, pi=pi, half=half, dt=dt: e.activation(
                    out=ke_h[:, half * 8:half * 8 + 8, dt * 128:(dt + 1) * 128],
                    in_=psb[pi][:, 0:1024].rearrange("p (a b) -> p a b", b=128), func=AF.Copy),
                    reads=[ps[pi]], writes=ke[half * 8:half * 8 + 8])

        b1_gate(0)
        for dt in range(8):
            if dt + 1 < 8:
                b1_gate(dt + 1)
            b1_apply(dt)
        P.flush()
        P.end_scope()
        P.scope()
        gnb = P.sbT([128, 512], F32, "gnb")
        dma(P, "sp", gnb, gnb[:, :], WT, bass.AP(gla_norm_g, 0, [[0, 128], [1, 512]]))
        vr = Rot([P.sbT([128, D], BF16, "v") for _ in range(3)])
        srr = Rot([P.sbT([128, D], BF16, "sr") for _ in range(3)])
        state = [[P.sbT([128, 512], F32, "st") for _ in range(2)] for _ in range(4)]
        stb = [[Rot([P.sbT([128, 512], BF16, "stb") for _ in range(2)]) for _ in range(2)] for _ in range(4)]
        stb_cur = [[None, None] for _ in range(4)]
        attm_r = Rot([P.sbT([128, 128], BF16, "attm") for _ in range(4)])
        tmp_r = Rot([P.sbT([128, 512], F32, "tmp") for _ in range(3)])
        og_r = Rot([P.sbT([128, D], BF16, "og") for _ in range(2)])
        ogT_r = Rot([P.sbT([128, KT, 128], BF16, "ogT") for _ in range(2)])
        junk = P.sbT([128, 512], BF16, "junk")
        stats = stat_tiles(64)
        ps_att = Rot([0])
        ps_o = Rot([1, 2, 7])
        ps_su = Rot([3, 4])
        ps_tr = Rot([5, 6])
        ogT_view = ogT_s.ap().rearrange("(et p) s -> p et s", p=128)
        items = [(t, h) for t in range(NT) for h in range(4)]
        vt, srt, ogt, att_of = {}, {}, {}, {}

        def load(t):
            v, sr = vr.next(), srr.next()
            dma(P, "sp", v, v[:, :], vs_t[t], vs_t[t][:, :])
            dma(P, "sp", sr, sr[:, :], srs_t[t], srs_t[t][:, :])
            vt[t], srt[t], ogt[t] = v, sr, og_r.next()

        def att_stage(i):
            t, h = items[i]
            tc = slice(t * 128, (t + 1) * 128)
            pa = ps[ps_att.next()]
            for dl in range(2):
                dt = 2 * h + dl
                mm(P, pa, pa[:, 0:128], ki[dt], ki[dt][:, tc], qd[dt], qd[dt][:, tc], dl == 0, dl == 1)
            attm = attm_r.next()
            tt(P, "dve", attm, attm[:, :], pa, pa[:, 0:128], maskT, maskT[:, :], ALU.mult)
            att_of[i] = attm

        def main_stage(i):
            t, h = items[i]
            tc = slice(t * 128, (t + 1) * 128)
            hc = slice(h * 512, (h + 1) * 512)
            v = vt[t]
            attm = att_of.pop(i)
            po = ps[ps_o.next()]
            mm(P, po, po[:, :], attm, attm[:, :], v, v[:, hc], True, t == 0)
            if t > 0:
                for dl in range(2):
                    dt = 2 * h + dl
                    sb_ = stb_cur[h][dl]
                    mm(P, po, po[:, :], qd[dt], qd[dt][:, tc], sb_, sb_[:, :], False, dl == 1)
            if t < NT - 1:
                for dl in range(2):
                    dt = 2 * h + dl
                    pu = ps[ps_su.next()]
                    mm(P, pu, pu[:, :], ke[t], ke_h[:, t, dt * 128:(dt + 1) * 128], v, v[:, hc], True, True)
                    st_ = state[h][dl]
                    if t == 0:
                        cp(P, "dve", st_, st_[:, :], pu, pu[:, :])
                    else:
                        stt(P, st_, st_[:, :], st_, st_[:, :], dec[dt][:, t:t + 1], pu, pu[:, :], ALU.mult, ALU.add,
                            extra_reads=[dec[dt]])
                    nb = stb[h][dl].next()
                    cp(P, "act", nb, nb[:, :], st_, st_[:, :])
                    stb_cur[h][dl] = nb
            return po

        def epi_a(i, po):
            st = stats[i]
            act(P, junk, junk[:, :], po, po[:, :], AF.Square, accum=(st, st[:, 0:1]))
            rstd_ops(P, st, st[:, 0:1], st[:, 1:2], st[:, 2:3], 512, nh, nh[:, 0:1])

        def epi_stage(i, po):
            t, h = items[i]
            hc = slice(h * 512, (h + 1) * 512)
            st = stats[i]
            og, sr = ogt[t], srt[t]
            tmp = tmp_r.next()
            stt(P, tmp, tmp[:, :], po, po[:, :], st[:, 2:3], gnb, gnb[:, :], ALU.mult, ALU.mult, extra_reads=[st])
            tt(P, "pool", og, og[:, hc], tmp, tmp[:, :], sr, sr[:, hc], ALU.mult)
            if h == 3:
                tc = slice(t * 128, (t + 1) * 128)
                ogT = ogT_r.next()
                for half in range(2):
                    pi = ps_tr.next()
                    for k8 in range(8):
                        et = half * 8 + k8
                        tr(P, ps[pi], psb[pi][:, k8 * 128:(k8 + 1) * 128], og, og[:, et * 128:(et + 1) * 128], ident, ident[:, :])
                    cp(P, "act" if half == 0 else "dve", ogT, ogT[:, half * 8:half * 8 + 8, :], ps[pi],
                       psb[pi][:, 0:1024].rearrange("p (a b) -> p a b", b=128))
                dma(P, "pool", ogTs_t, ogT_view[:, :, tc], ogT, ogT[:, :, :])

        load(0)
        att_stage(0)
        prev = None
        for i in range(len(items)):
            t, h = items[i]
            if h == 0 and t + 1 < NT:
                load(t + 1)
            if i + 1 < len(items):
                att_stage(i + 1)
            if prev is not None:
                epi_a(*prev)
            po = main_stage(i)
            if prev is not None:
                epi_stage(*prev)
            prev = (i, po)
        epi_a(*prev)
        epi_stage(*prev)
        P.flush()
        P.end_scope()
        P.end_scope()

    def phase_linpost(inT_dram, inT_t, W2d, g1d, xold_t, xnew_t):
        P.scope()
        gb = load_gb(g1d)
        inT = P.sbT([128, KT, S], BF16, "inT")
        src = inT_dram.ap().rearrange("(kt p) s -> p kt s", p=128)
        inTq = [P.T(inT[:, :, q * 512:(q + 1) * 512]) for q in range(4)]
        for q in range(4):
            dma(P, "sp", inTq[q], inTq[q][:, :, :], inT_t, src[:, :, q * 512:(q + 1) * 512])
        slabs = [P.sbT([128, KT, 512], BF16, "w") for _ in range(4)]
        for cc in range(4):
            wslab(W2d, cc * 512, 512, slabs[cc], KT)
        xr = Rot([P.sbT([128, D], F32, "xo") for _ in range(2)])
        tmpr = Rot([P.sbT([128, D], F32, "tmp") for _ in range(2)])
        junk = P.sbT([128, 512], BF16, "junk")
        stats = stat_tiles(NT, 8)
        for t in range(NT):
            stat = stats[t]
            base = (t % 2) * 4
            xo = xr.next()
            dma(P, "sp", xo, xo[:, :], xold_t[t], xold_t[t][:, :])
            iq = inTq[t // 4]
            for cc in range(4):
                pt = ps[base + cc]
                for kt in range(KT):
                    mm(P, pt, pt[:, :], iq, inT[:, kt, t * 128:(t + 1) * 128], slabs[cc], slabs[cc][:, kt, :], kt == 0, kt == KT - 1)
                act(P, junk, junk[:, :], pt, pt[:, :], AF.Square, accum=(stat, stat[:, cc:cc + 1]))
            o = 0
            tt(P, "pool", stat, stat[:, o + 4:o + 5], stat, stat[:, o:o + 1], stat, stat[:, o + 1:o + 2], ALU.add)
            tt(P, "pool", stat, stat[:, o + 5:o + 6], stat, stat[:, o + 2:o + 3], stat, stat[:, o + 3:o + 4], ALU.add)
            tt(P, "pool", stat, stat[:, o + 4:o + 5], stat, stat[:, o + 4:o + 5], stat, stat[:, o + 5:o + 6], ALU.add)
            rstd_ops(P, stat, stat[:, o + 4:o + 5], stat[:, o + 6:o + 7], stat[:, o + 7:o + 8], D, nh, nh[:, 0:1])
            tmp = tmpr.next()
            for cc in range(4):
                pt = ps[base + cc]
                cs = slice(cc * 512, (cc + 1) * 512)
                stt(P, tmp, tmp[:, cs], pt, pt[:, :], stat[:, o + 7:o + 8], gb, gb[:, cs], ALU.mult, ALU.mult, extra_reads=[stat])
            tt(P, "pool", tmp, tmp[:, :], tmp, tmp[:, :], xo, xo[:, :], ALU.add)
            dma(P, "pool", xnew_t[t], xnew_t[t][:, :], tmp, tmp[:, :])
        P.flush()
        P.end_scope()

    def phase_ffn(layer, xold_t, xnew_t, do_d4=True):
        Wgu = ffn_w_gate_up[layer, :, :]
        Wd = ffn_w_down[layer, :, :]
        TB = 1024
        P.scope()
        for blk in range(S // TB):
            P.scope()
            hid = P.sbT([128, FT, TB], BF16, "hid")
            hidc = [P.T(hid[:, :, c * 512:(c + 1) * 512]) for c in range(2)]
            P.scope()
            gb = load_gb(pre_ffn_g[layer, :])
            hT = P.sb([128, KT, TB], BF16, "hT")
            hTc = chunk_tiles(hT, TB)
            xpool = Rot([P.sbT([128, D], F32, "xt") for _ in range(2)])
            xspool = Rot([P.sbT([128, D], BF16, "xs") for _ in range(3)])
            junk = P.sbT([128, D], BF16, "junk")
            prenorm_T(xold_t[blk * 8:(blk + 1) * 8], gb, hT, hTc, 8, xpool, xspool, junk, stat_tiles(8), Rot([0, 1]))
            SW = 256
            slabG = Rot([P.sbT([128, KT, SW], BF16, "sg") for _ in range(2)])
            slabU = Rot([P.sbT([128, KT, SW], BF16, "su") for _ in range(2)])
            sgr = Rot([P.sbT([128, 512], F32, "sil") for _ in range(3)])
            psg = Rot([2, 3, 4])
            psu = Rot([5, 6, 7])
            for s_ in range(DFF // SW):
                sg_, su_ = slabG.next(), slabU.next()
                wslab(Wgu, s_ * SW, SW, sg_, KT)
                wslab(Wgu, DFF + s_ * SW, SW, su_, KT)
                for f in range(SW // 128):
                    j = s_ * (SW // 128) + f
                    for c in range(2):
                        pg, pu = ps[psg.next()], ps[psu.next()]
                        for kt in range(KT):
                            mm(P, pg, pg[:, :], sg_, sg_[:, kt, f * 128:(f + 1) * 128], hTc[c], hT[:, kt, c * 512:(c + 1) * 512],
                               kt == 0, kt == KT - 1)
                        for kt in range(KT):
                            mm(P, pu, pu[:, :], su_, su_[:, kt, f * 128:(f + 1) * 128], hTc[c], hT[:, kt, c * 512:(c + 1) * 512],
                               kt == 0, kt == KT - 1)
                        sil = sgr.next()
                        act(P, sil, sil[:, :], pg, pg[:, :], AF.Silu)
                        tt(P, "dve", hidc[c], hid[:, j, c * 512:(c + 1) * 512], sil, sil[:, :], pu, pu[:, :], ALU.mult)
            P.flush()
            P.end_scope()
            P.scope()
            def mk_sd():
                h = P.sb([128, FT, 512], BF16, "sd")
                return (h, [P.T(h[:, p * 11:(p + 1) * 11, :]) for p in range(4)])
            slabD = Rot([mk_sd() for _ in range(2)])
            fevr = Rot([P.sbT([128, 512], F32, "fev") for _ in range(4)])
            junk = P.sbT([128, 512], BF16, "junk")
            psf = Rot([int(c) for c in os.environ.get("KBANKS", "01234567")] if debug else [0, 1, 2, 3, 4, 5, 6, 7])
            Wd_v = Wd.rearrange("(j p) n -> p j n", p=128)
            for cc in range(4):
                sd, sdp = slabD.next()
                for p in range(4):
                    dma(P, "pool", sdp[p], sd[:, p * 11:(p + 1) * 11, :], WT, Wd_v[:, p * 11:(p + 1) * 11, cc * 512:(cc + 1) * 512])
                for tl in range(8):
                    t = blk * 8 + tl
                    pf = ps[psf.next()]
                    for j in range(FT if "mm" not in SKIP else 0):
                        mm(P, pf, pf[:, :], hidc[tl // 4], hid[:, j, tl * 128:(tl + 1) * 128], sdp[j // 11], sd[:, j, :], j == 0, j == FT - 1)
                    st = fstats[t]
                    if "sq" not in SKIP:
                        act(P, junk, junk[:, :], pf, pf[:, :], AF.Square, accum=(st, st[:, cc:cc + 1]))
                    fev = fevr.next()
                    if "cp" not in SKIP:
                        cp(P, "dve", fev, fev[:, :], pf, pf[:, :])
                    if "st" not in SKIP:
                        dma(P, "sp", fscr_t[t], fscr_t[t][:, cc * 512:(cc + 1) * 512], fev, fev[:, :])
            P.flush()
            P.end_scope()
            P.end_scope()
        if not do_d4:
            P.end_scope()
            return
        P.scope()
        gb = load_gb(post_ffn_g[layer, :])
        xr = Rot([P.sbT([128, D], F32, "xo") for _ in range(3)])
        fr = Rot([P.sbT([128, D], F32, "fo") for _ in range(3)])
        for t in range(0 if "d4" not in SKIP else NT, NT):
            xo, fo = xr.next(), fr.next()
            dma(P, "sp", xo, xo[:, :], xold_t[t], xold_t[t][:, :])
            dma(P, "sp", fo, fo[:, :], fscr_t[t], fscr_t[t][:, :])
            stat = fstats[t]
            tt(P, "pool", stat, stat[:, 4:5], stat, stat[:, 0:1], stat, stat[:, 1:2], ALU.add)
            tt(P, "pool", stat, stat[:, 5:6], stat, stat[:, 2:3], stat, stat[:, 3:4], ALU.add)
            tt(P, "pool", stat, stat[:, 4:5], stat, stat[:, 4:5], stat, stat[:, 5:6], ALU.add)
            rstd_ops(P, stat, stat[:, 4:5], stat[:, 6:7], stat[:, 7:8], D, nh, nh[:, 0:1])
            stt(P, fo, fo[:, :], fo, fo[:, :], stat[:, 7:8], gb, gb[:, :], ALU.mult, ALU.mult, extra_reads=[stat])
            tt(P, "dve", fo, fo[:, :], fo, fo[:, :], xo, xo[:, :], ALU.add)
            dma(P, "pool", xnew_t[t], xnew_t[t][:, :], fo, fo[:, :])
        P.flush()
        P.end_scope()
        P.end_scope()

    def phase_E(fuse_ffn=True):
        for which in range(2):
            P.scope()
            gb = load_gb(kv_norm_g.ap() if which == 0 else pre_mix_g[1, :])
            hT = P.sb([128, KT, S], BF16, "hT")
            hTc = chunk_tiles(hT, S)
            xspool = Rot([P.sbT([128, D], BF16, "xs") for _ in range(3)])
            junk = P.sbT([128, D], BF16, "junk")
            if which == 0 and fuse_ffn:
                xpool = Rot([P.sbT([128, D], F32, "xt") for _ in range(3)])
                fuse = {"fpool": Rot([P.sbT([128, D], F32, "ft") for _ in range(2)]), "f": fscr_t, "stats": fstats,
                        "gpost": load_gb(post_ffn_g[0, :]), "xnew": x2_t}
                prenorm_T(x1_t, gb, hT, hTc, NT, xpool, xspool, junk, stat_tiles(NT), Rot([0, 1]), fuse=fuse)
            else:
                xpool = Rot([P.sbT([128, D], F32, "xt") for _ in range(2)])
                prenorm_T(x2_t, gb, hT, hTc, NT, xpool, xspool, junk, stat_tiles(NT), Rot([0, 1]))
            W = w_kv[:, :] if which == 0 else diff_w_q[:, :]
            dstT = kT2_t if which == 0 else qT2_t
            scale = 1.0 if which == 0 else 128.0 ** -0.5
            psr = Rot([2, 3, 4, 5, 6, 7])
            if which == 0:
                slabsA = Rot([P.sbT([128, KT, 512], BF16, "slabA") for _ in range(2)])
                evA = Rot([P.sbT([128, 512], BF16, "evA") for _ in range(4)])
                for ch in range(4):
                    slab = slabsA.next()
                    wslab(W, 2048 + ch * 512, 512, slab, KT)
                    for t in range(NT):
                        pt = ps[psr.next()]
                        for kt in range(KT):
                            mm(P, pt, pt[:, :], hTc[t // 4], hT[:, kt, t * 128:(t + 1) * 128], slab, slab[:, kt, :], kt == 0, kt == KT - 1)
                        ev = evA.next()
                        cp(P, "dve" if t % 2 == 0 else "act", ev, ev[:, :], pt, pt[:, :])
                        dma(P, "sp", v2_t[t], v2_t[t][:, ch * 512:(ch + 1) * 512], ev, ev[:, :])
            slabsB = Rot([P.sbT([128, KT, 256], BF16, "slabB") for _ in range(2)])
            evB = Rot([P.sbT([128, S], BF16, "evB") for _ in range(2)])
            for sidx in range(8):
                slab = slabsB.next()
                wslab(W, sidx * 256, 256, slab, KT)
                for n in range(2):
                    nt_idx = sidx * 2 + n
                    ev = evB.next()
                    for c in range(4):
                        pt = ps[psr.next()]
                        for kt in range(KT):
                            mm(P, pt, pt[:, :], slab, slab[:, kt, n * 128:(n + 1) * 128], hTc[c], hT[:, kt, c * 512:(c + 1) * 512],
                               kt == 0, kt == KT - 1)
                        if c % 2 == 0:
                            act(P, ev, ev[:, c * 512:(c + 1) * 512], pt, pt[:, :], AF.Copy, scale=scale)
                        else:
                            ts(P, "dve", ev, ev[:, c * 512:(c + 1) * 512], pt, pt[:, :], scale, None, ALU.mult)
                    dma(P, "sp", dstT[nt_idx], dstT[nt_idx][:, :], ev, ev[:, :])
            P.flush()
            P.end_scope()

    def phase_F():
        P.scope()
        Bh = P.sbT([128, 8, 256], F32, "Bh")
        dma(P, "sp", Bh, Bh[:, :, :], WT, bias_t[:, :, :])
        for h in range(8):
            P.add("pool", lambda e, h=h: e.affine_select(out=Bh[:, h, 0:128], in_=Bh[:, h, 0:128], pattern=[[1, 128]],
                                                         compare_op=ALU.is_ge, fill=MASKV, base=0, channel_multiplier=-1),
                  reads=[Bh], writes=[Bh])
        kTr = Rot([P.sbT([128, 2, S], BF16, "kTh") for _ in range(2)])
        qTr = Rot([P.sbT([128, 2, S], BF16, "qTh") for _ in range(2)])
        Vr = Rot([P.sbT([128, NT, 256], BF16, "Vh") for _ in range(2)])
        PTr = Rot([P.sbT([128, 512], BF16, "PT") for _ in range(6)])
        tbr = Rot([P.sbT([128, 256], F32, "tb") for _ in range(4)])
        rlr = [Rot([P.sbT([128, 512], F32, "rl") for _ in range(2)]) for _ in range(2)]
        ar = [Rot([P.sbT([128, 512], F32, "a") for _ in range(2)]) for _ in range(2)]
        t1r = Rot([P.sbT([128, 512], F32, "t1") for _ in range(4)])
        sqr = Rot([P.sbT([128, 512], BF16, "sq") for _ in range(4)])
        rsr = Rot([P.sbT([128, 512], F32, "rs") for _ in range(2)])
        oTr = Rot([P.sbT([128, 512], BF16, "oT") for _ in range(4)])
        ps_s = Rot([6, 7])
        psO = [[ps[0], ps[1]], [ps[2], ps[3]]]
        psL = [ps[4], ps[5]]
        kv_view = kT2.ap().rearrange("(n p) s -> p n s", p=128)
        q_view = qT2.ap().rearrange("(n p) s -> p n s", p=128)
        v_view = v2.ap().rearrange("(t p) e -> p t e", p=128)
        kT2_all = P.T(kT2[:, :])
        qT2_all = P.T(qT2[:, :])
        v2_all = P.T(v2[:, :])
        head_tiles = {}

        def load_head(h):
            kTh, qTh, Vh = kTr.next(), qTr.next(), Vr.next()
            dma(P, "sp", kTh, kTh[:, :, :], kT2_all, kv_view[:, 2 * h:2 * h + 2, :])
            dma(P, "sp", qTh, qTh[:, :, :], qT2_all, q_view[:, 2 * h:2 * h + 2, :])
            dma(P, "sp", Vh, Vh[:, :, :], v2_all, v_view[:, :, h * 256:(h + 1) * 256])
            head_tiles[h] = (kTh, qTh, Vh)

        steps = []
        for h in range(8):
            for qb in range(4):
                nj = 4 * qb + 4
                for j in range(nj):
                    for m in range(2):
                        steps.append((h, qb, j, m, nj))

        def qk_exp(s_):
            h, qb, j, m, nj = steps[s_]
            kTh, qTh, Vh = head_tiles[h]
            c0 = max(0, (j - 4 * qb) * 128)
            if j >= 4 * qb:
                nb = 2 if (j - 4 * qb) < 3 else 1
                b0 = 0
            elif j == 4 * qb - 1:
                nb, b0 = 1, 128
            else:
                nb, b0 = 0, 0
            pS = ps[ps_s.next()]
            mm(P, pS, pS[:, c0:512], kTh, kTh[:, m, j * 128:(j + 1) * 128], qTh, qTh[:, m, qb * 512 + c0:(qb + 1) * 512], True, True)
            PT = PTr.next()
            c1 = c0 + nb * 128
            if nb > 0:
                tb = tbr.next()
                tt(P, "dve", tb, tb[:, 0:nb * 128], pS, pS[:, c0:c1], Bh, Bh[:, h, b0:b0 + nb * 128], ALU.add)
            if c1 < 512:
                act(P, PT, PT[:, c1:512], pS, pS[:, c1:512], AF.Exp, bias=ctab_b[:, h:h + 1], extra_reads=[ctab_b])
            if nb > 0:
                act(P, PT, PT[:, c0:c1], tb, tb[:, 0:nb * 128], AF.Exp)
            return PT, c0

        def pv(s_, PT, c0):
            h, qb, j, m, nj = steps[s_]
            kTh, qTh, Vh = head_tiles[h]
            for e_ in range(2):
                po = psO[m][e_]
                mm(P, po, po[:, c0:512], Vh, Vh[:, j, e_ * 128:(e_ + 1) * 128], PT, PT[:, c0:512], j == 0, j == nj - 1)
            mm(P, psL[m], psL[m][:, c0:512], ones_b, ones_b[:, :], PT, PT[:, c0:512], j == 0, j == nj - 1)

        deferred = []

        def finalize(h, qb, s_now):
            rl = [rlr[0].next(), rlr[1].next()]
            for m in range(2):
                P.add("dve", lambda e, o=rl[m], i=psL[m]: e.reciprocal(out=o[:, :], in_=i[:, :]), reads=[psL[m]], writes=[rl[m]])
            a = [ar[0].next(), ar[1].next()]
            t1 = [t1r.next(), t1r.next()]
            for e_ in range(2):
                stt(P, t1[e_], t1[e_][:, :], psO[1][e_], psO[1][e_][:, :], neglam[:, 0:1], rl[1], rl[1][:, :], ALU.mult, ALU.mult,
                    extra_reads=[neglam])
                tt(P, "dve", a[e_], a[e_][:, :], psO[0][e_], psO[0][e_][:, :], rl[0], rl[0][:, :], ALU.mult)
            sq = [sqr.next(), sqr.next()]
            rs = rsr.next()

            def part2a():
                for e_ in range(2):
                    tt(P, "pool", a[e_], a[e_][:, :], a[e_], a[e_][:, :], t1[e_], t1[e_][:, :], ALU.add)
                    act(P, sq[e_], sq[e_][:, :], a[e_], a[e_][:, :], AF.Square)

            def part2b():
                pS = ps[ps_s.next()]
                for e_ in range(2):
                    mm(P, pS, pS[:, :], ones_b, ones_b[:, :], sq[e_], sq[e_][:, :], e_ == 0, e_ == 1)
                act(P, rs, rs[:, :], pS, pS[:, :], AF.Ln, scale=1.0 / 256, bias=EPS)
                act(P, rs, rs[:, :], rs, rs[:, :], AF.Exp, scale=-0.5)

            def part2c():
                for e_ in range(2):
                    oT = oTr.next()
                    stt(P, oT, oT[:, :], a[e_], a[e_][:, :], gsub[:, e_:e_ + 1], rs, rs[:, :], ALU.mult, ALU.mult, extra_reads=[gsub])
                    r0 = (2 * h + e_) * 128
                    dma(P, "pool", oTs_t, oT_s[r0:r0 + 128, qb * 512:(qb + 1) * 512], oT, oT[:, :])

            deferred.append((s_now + 1, part2a))
            deferred.append((s_now + 2, part2b))
            deferred.append((s_now + 4, part2c))

        pend = {}

        def do_pv(sp_, tick):
            h, qb, j, m, nj = steps[sp_]
            pv(sp_, *pend.pop(sp_))
            if j == nj - 1 and m == 1:
                finalize(h, qb, tick)

        load_head(0)
        pend[0] = qk_exp(0)
        n_steps = len(steps)
        for s_ in range(n_steps):
            h, qb, j, m, nj = steps[s_]
            if s_ + 1 < n_steps:
                pend[s_ + 1] = qk_exp(s_ + 1)
            if s_ >= 1:
                do_pv(s_ - 1, s_)
            if qb == 0 and j == 0 and m == 1 and h + 1 < 8:
                load_head(h + 1)
            while deferred and deferred[0][0] <= s_:
                deferred.pop(0)[1]()
        do_pv(n_steps - 1, n_steps)
        while deferred:
            deferred.pop(0)[1]()
        P.flush()
        P.end_scope()

    import os
    stop = int(os.environ.get("KSTOP", "99")) if debug else 99
    steps = [
        phase_A,
        phase_B,
        lambda: phase_linpost(ogT_s, ogTs_t, gla_w_out[:, :], post_mix_g[0, :], xin_t, x1_t),
        lambda: phase_ffn(0, x1_t, x2_t, do_d4=False),
        phase_E,
        phase_F,
        lambda: phase_linpost(oT_s, oTs_t, diff_w_out[:, :], post_mix_g[1, :], x2_t, x3_t),
        lambda: phase_ffn(1, x3_t, y_t),
    ]
    for i, st_ in enumerate(steps):
        if i < stop:
            st_()
    print("instructions:", P.n_inst)
    return nc


def _t5_bucket_np(dist):
    n = np.maximum(dist, 0)
    nf = np.maximum(n, 1).astype(np.float32)
    large = 16 + (np.log(nf / 16) / np.float32(math.log(128 / 16)) * 16).astype(np.int32)
    large = np.minimum(large, 31)
    return np.where(n < 16, n, large)


def prep_inputs(inputs):
    f = lambda a: np.ascontiguousarray(np.asarray(a, dtype=np.float32))
    table = f(inputs["rel_bias_table"])
    k = np.arange(128)[:, None]
    c = np.arange(256)[None, :]
    idx = _t5_bucket_np(c - k)
    bias_t = np.ascontiguousarray(np.transpose(table[idx], (0, 2, 1)))
    shared = {
        "bias_t": bias_t,
        "ctab": f(table[31, :]),
        "kv_norm_g": f(inputs["kv_norm_g"]),
        "w_kv": f(inputs["w_kv"]),
        "gla_w_in": f(inputs["gla_w_in"][0]),
        "gla_w_fgate": f(inputs["gla_w_fgate"][0]),
        "gla_b_fgate": f(inputs["gla_b_fgate"][0]),
        "gla_norm_g": f(inputs["gla_norm_g"][0]),
        "gla_w_out": f(inputs["gla_w_out"][0]),
        "diff_w_q": f(inputs["diff_w_q"][0]),
        "lam4": f(np.stack([inputs["diff_lam_q1"][0], inputs["diff_lam_k1"][0], inputs["diff_lam_q2"][0], inputs["diff_lam_k2"][0]])),
        "diff_subln_g": f(inputs["diff_subln_g"][0]),
        "diff_w_out": f(inputs["diff_w_out"][0]),
        "pre_mix_g": f(inputs["pre_mix_g"]),
        "post_mix_g": f(inputs["post_mix_g"]),
        "pre_ffn_g": f(inputs["pre_ffn_g"]),
        "post_ffn_g": f(inputs["post_ffn_g"]),
        "ffn_w_gate_up": f(inputs["ffn_w_gate_up"]),
        "ffn_w_down": f(inputs["ffn_w_down"]),
    }
    x = f(inputs["x"])
    return [dict(shared, x=x[b]) for b in range(x.shape[0])]


_NC_CACHE = {}


def kernel(**inputs):
    in_maps = prep_inputs(inputs)
    if "nc" not in _NC_CACHE:
        _NC_CACHE["nc"] = build_nc()
    nc = _NC_CACHE["nc"]
    res = run_bass_kernel_spmd(nc, in_maps, core_ids=list(range(8)))
    return np.stack([np.asarray(r["y"], dtype=np.float32) for r in res.results], axis=0)
```
